# Optimizing a Trainium2 kernel written in Bass

```python
import math
import jax, jax.numpy as jnp
from jax import lax
import numpy as np

D_MODEL = 2048
BATCH = 4
SEQ = 2048
DEPTH = 2
DEC_BATCH = 128
DEC_SEQ = 4
PAST_LEN = 8192
PAGE_SIZE = 128

MIX_W = D_MODEL
SSM_W = MIX_W // 2
ATTN_W = MIX_W - SSM_W
SSM_GROUP_CH = 16
SSM_GROUPS = SSM_W // SSM_GROUP_CH
SSM_STATE = 64
HEAD_DIM = 64
N_HEADS = ATTN_W // HEAD_DIM
N_KV_HEADS = 4
Q_PER_KV = N_HEADS // N_KV_HEADS
KV_W = N_KV_HEADS * HEAD_DIM
IN_W = SSM_W + ATTN_W + 2 * KV_W
WINDOW = 128
BLOCK = WINDOW
CACHE_W = min(WINDOW, PAST_LEN)
PROMPT_CACHE_W = min(WINDOW, SEQ)
D_FF = ((8 * D_MODEL // 3 + 255) // 256) * 256
EPS = 1e-6

kernel_name = "hymba_s5_swa_macaron_step"


def rms_norm(x, g):
    x32 = x.astype(jnp.float32)
    y = x32 * lax.rsqrt(jnp.mean(x32 * x32, axis=-1, keepdims=True) + EPS)
    return (y * g.astype(jnp.float32)).astype(x.dtype)


def swiglu(x, w_gate, w_up, w_down):
    return (jax.nn.silu(x @ w_gate) * (x @ w_up)) @ w_down


def alibi_slopes():
    h = jnp.arange(1, N_HEADS + 1, dtype=jnp.float32)
    return jnp.exp2(-8.0 * h / N_HEADS)


def _ssm_combine(e1, e2):
    a1, b1 = e1
    a2, b2 = e2
    return a1 * a2, a2 * b1 + b2


def s5_mixer(u, h0, A_re, A_im, B_re, B_im, C_re, C_im, D, log_dt, w_glu):
    f32 = jnp.float32
    lam = lax.complex(A_re.astype(f32), A_im.astype(f32))
    dt = jnp.exp(log_dt.astype(f32))[:, None]
    lam_bar = jnp.exp(lam * dt)
    Bc = lax.complex(B_re.astype(f32), B_im.astype(f32))
    B_bar = ((lam_bar - 1.0) / lam)[..., None] * Bc
    Cc = lax.complex(C_re.astype(f32), C_im.astype(f32))
    u32 = u.astype(f32)
    bu = jnp.einsum('blgc,gpc->blgp', u32.astype(jnp.complex64), B_bar)
    bu = bu.at[:, 0].add(lam_bar * h0)
    a = jnp.broadcast_to(lam_bar, bu.shape)
    _, h = lax.associative_scan(_ssm_combine, (a, bu), axis=1)
    y = jnp.einsum('blgp,gcp->blgc', h, Cc).real + D.astype(f32) * u32
    y = jax.nn.gelu(y)
    y = y * jax.nn.sigmoid(jnp.einsum('blgc,gcd->blgd', y, w_glu.astype(f32)))
    return y, h[:, -1]


def window_attention(q, k, v, q_pos, k_pos, sinks):
    b, n, lq = q.shape[:3]
    qg = q.reshape(b, n, lq, N_KV_HEADS, Q_PER_KV, HEAD_DIM)
    s = jnp.einsum('bnqkgd,bnskd->bnkgqs', qg, k).astype(jnp.float32) * (HEAD_DIM ** -0.5)
    dist = q_pos[:, :, None] - k_pos[:, None, :]
    valid = (dist >= 0) & (dist < WINDOW) & (k_pos[:, None, :] >= 0)
    slopes = alibi_slopes().reshape(N_KV_HEADS, Q_PER_KV)
    bias = -slopes[None, :, :, None, None] * dist[:, None, None].astype(jnp.float32)
    logits = jnp.where(valid[:, None, None], s + bias, -jnp.inf)
    sink = sinks.astype(jnp.float32).reshape(N_KV_HEADS, Q_PER_KV)[:, :, None, None]
    m = jnp.maximum(logits.max(axis=-1, keepdims=True), sink)
    p = jnp.exp(logits - m)
    p = p / (p.sum(axis=-1, keepdims=True) + jnp.exp(sink - m))
    o = jnp.einsum('bnkgqs,bnskd->bnqkgd', p.astype(v.dtype), v)
    return o.reshape(b, n * lq, N_HEADS * HEAD_DIM)


def decoder_layer(x, lp, h0, k_past, v_past):
    bsz, L, _ = x.shape
    x = x + 0.5 * swiglu(rms_norm(x, lp['ffn1_norm']), lp['ffn1_w_gate'], lp['ffn1_w_up'], lp['ffn1_w_down'])
    hn = rms_norm(x, lp['mix_norm'])
    z = hn @ lp['w_in']
    u, q, k, v = jnp.split(z, [SSM_W, SSM_W + ATTN_W, SSM_W + ATTN_W + KV_W], axis=-1)
    y_ssm, h_last = s5_mixer(u.reshape(bsz, L, SSM_GROUPS, SSM_GROUP_CH), h0,
                             lp['ssm_A_re'], lp['ssm_A_im'], lp['ssm_B_re'], lp['ssm_B_im'],
                             lp['ssm_C_re'], lp['ssm_C_im'], lp['ssm_D'], lp['ssm_log_dt'], lp['ssm_w_glu'])
    y_ssm = y_ssm.reshape(bsz, L, SSM_W).astype(x.dtype)
    q = rms_norm(q.reshape(bsz, L, N_HEADS, HEAD_DIM), lp['q_norm'])
    k = rms_norm(k.reshape(bsz, L, N_KV_HEADS, HEAD_DIM), lp['k_norm'])
    v = v.reshape(bsz, L, N_KV_HEADS, HEAD_DIM)
    if k_past is None:
        nb = L // BLOCK
        qb = q.reshape(bsz, nb, BLOCK, N_HEADS, HEAD_DIM)
        kb = k.reshape(bsz, nb, BLOCK, N_KV_HEADS, HEAD_DIM)
        vb = v.reshape(bsz, nb, BLOCK, N_KV_HEADS, HEAD_DIM)
        kk = jnp.concatenate([jnp.concatenate([jnp.zeros_like(kb[:, :1]), kb[:, :-1]], axis=1), kb], axis=2)
        vv = jnp.concatenate([jnp.concatenate([jnp.zeros_like(vb[:, :1]), vb[:, :-1]], axis=1), vb], axis=2)
        q_pos = jnp.arange(L, dtype=jnp.int32).reshape(nb, BLOCK)
        k_pos = jnp.concatenate([q_pos - BLOCK, q_pos], axis=1)
        y_attn = window_attention(qb, kk, vv, q_pos, k_pos, lp['sinks'])
        k_state = k[:, L - PROMPT_CACHE_W:]
        v_state = v[:, L - PROMPT_CACHE_W:]
    else:
        kk = jnp.concatenate([k_past.astype(k.dtype), k], axis=1)
        vv = jnp.concatenate([v_past.astype(v.dtype), v], axis=1)
        q_pos = (PAST_LEN + jnp.arange(L, dtype=jnp.int32))[None]
        k_pos = jnp.concatenate([PAST_LEN - CACHE_W + jnp.arange(CACHE_W, dtype=jnp.int32),
                                 q_pos[0]])[None]
        y_attn = window_attention(q[:, None], kk[:, None], vv[:, None], q_pos, k_pos, lp['sinks'])
        k_state = kk[:, L:]
        v_state = vv[:, L:]
    y_attn = y_attn.astype(x.dtype)
    merged = jnp.concatenate([rms_norm(y_ssm, lp['ssm_out_norm']), rms_norm(y_attn, lp['attn_out_norm'])], axis=-1)
    x = x + merged @ lp['w_out']
    x = x + 0.5 * swiglu(rms_norm(x, lp['ffn2_norm']), lp['ffn2_w_gate'], lp['ffn2_w_up'], lp['ffn2_w_down'])
    return x, k_state, v_state, h_last


def setup_inputs(seed: int = 0) -> dict:
    key = jax.random.key(seed)
    ks = jax.random.split(key, 40)
    f32 = jnp.float32
    nrm = lambda i, shape, s: jax.random.normal(ks[i], shape, f32) * s
    gain = lambda i, shape: 1.0 + 0.05 * jax.random.normal(ks[i], shape, f32)
    a_im = math.pi * jnp.arange(SSM_STATE, dtype=f32)[None, None, :] + nrm(14, (DEPTH, SSM_GROUPS, SSM_STATE), 0.01)
    log_dt = jax.random.uniform(ks[21], (DEPTH, SSM_GROUPS), f32, math.log(1e-3), math.log(1e-1))
    return {
        "x_prompt": nrm(0, (BATCH, SEQ, D_MODEL), 1.0),
        "x_sample": nrm(1, (DEC_BATCH, DEC_SEQ, D_MODEL), 1.0),
        "cache_k": nrm(2, (DEPTH, DEC_BATCH, CACHE_W, N_KV_HEADS, HEAD_DIM), 1.0),
        "cache_v": nrm(3, (DEPTH, DEC_BATCH, CACHE_W, N_KV_HEADS, HEAD_DIM), 1.0),
        "state_ssm_re": nrm(4, (DEPTH, DEC_BATCH, SSM_GROUPS, SSM_STATE), 0.5),
        "state_ssm_im": nrm(5, (DEPTH, DEC_BATCH, SSM_GROUPS, SSM_STATE), 0.5),
        "ffn1_norm": gain(6, (DEPTH, D_MODEL)),
        "ffn1_w_gate": nrm(7, (DEPTH, D_MODEL, D_FF), D_MODEL ** -0.5),
        "ffn1_w_up": nrm(8, (DEPTH, D_MODEL, D_FF), D_MODEL ** -0.5),
        "ffn1_w_down": nrm(9, (DEPTH, D_FF, D_MODEL), D_FF ** -0.5),
        "mix_norm": gain(10, (DEPTH, D_MODEL)),
        "w_in": nrm(11, (DEPTH, D_MODEL, IN_W), D_MODEL ** -0.5),
        "ssm_A_re": -0.5 + nrm(13, (DEPTH, SSM_GROUPS, SSM_STATE), 0.01),
        "ssm_A_im": a_im,
        "ssm_B_re": nrm(15, (DEPTH, SSM_GROUPS, SSM_STATE, SSM_GROUP_CH), (2 * SSM_GROUP_CH) ** -0.5),
        "ssm_B_im": nrm(16, (DEPTH, SSM_GROUPS, SSM_STATE, SSM_GROUP_CH), (2 * SSM_GROUP_CH) ** -0.5),
        "ssm_C_re": nrm(17, (DEPTH, SSM_GROUPS, SSM_GROUP_CH, SSM_STATE), (2 * SSM_STATE) ** -0.5),
        "ssm_C_im": nrm(18, (DEPTH, SSM_GROUPS, SSM_GROUP_CH, SSM_STATE), (2 * SSM_STATE) ** -0.5),
        "ssm_D": nrm(19, (DEPTH, SSM_GROUPS, SSM_GROUP_CH), 1.0),
        "ssm_log_dt": log_dt,
        "ssm_w_glu": nrm(22, (DEPTH, SSM_GROUPS, SSM_GROUP_CH, SSM_GROUP_CH), SSM_GROUP_CH ** -0.5),
        "q_norm": gain(23, (DEPTH, HEAD_DIM)),
        "k_norm": gain(24, (DEPTH, HEAD_DIM)),
        "sinks": nrm(25, (DEPTH, N_HEADS), 0.5),
        "ssm_out_norm": gain(26, (DEPTH, SSM_W)),
        "attn_out_norm": gain(27, (DEPTH, ATTN_W)),
        "w_out": nrm(28, (DEPTH, MIX_W, D_MODEL), MIX_W ** -0.5),
        "ffn2_norm": gain(29, (DEPTH, D_MODEL)),
        "ffn2_w_gate": nrm(30, (DEPTH, D_MODEL, D_FF), D_MODEL ** -0.5),
        "ffn2_w_up": nrm(31, (DEPTH, D_MODEL, D_FF), D_MODEL ** -0.5),
        "ffn2_w_down": nrm(32, (DEPTH, D_FF, D_MODEL), D_FF ** -0.5),
    }


def reference(x_prompt, x_sample, cache_k, cache_v, state_ssm_re, state_ssm_im,
              ffn1_norm, ffn1_w_gate, ffn1_w_up, ffn1_w_down, mix_norm, w_in,
              ssm_A_re, ssm_A_im, ssm_B_re, ssm_B_im, ssm_C_re, ssm_C_im, ssm_D, ssm_log_dt, ssm_w_glu,
              q_norm, k_norm, sinks, ssm_out_norm, attn_out_norm, w_out,
              ffn2_norm, ffn2_w_gate, ffn2_w_up, ffn2_w_down):
    xp, xs = x_prompt, x_sample
    pk, pv, pre, pim, sk, sv, sre, sim = [], [], [], [], [], [], [], []
    for l in range(DEPTH):
        lp = dict(ffn1_norm=ffn1_norm[l], ffn1_w_gate=ffn1_w_gate[l], ffn1_w_up=ffn1_w_up[l],
                  ffn1_w_down=ffn1_w_down[l], mix_norm=mix_norm[l], w_in=w_in[l],
                  ssm_A_re=ssm_A_re[l], ssm_A_im=ssm_A_im[l], ssm_B_re=ssm_B_re[l], ssm_B_im=ssm_B_im[l],
                  ssm_C_re=ssm_C_re[l], ssm_C_im=ssm_C_im[l], ssm_D=ssm_D[l], ssm_log_dt=ssm_log_dt[l],
                  ssm_w_glu=ssm_w_glu[l], q_norm=q_norm[l], k_norm=k_norm[l], sinks=sinks[l],
                  ssm_out_norm=ssm_out_norm[l], attn_out_norm=attn_out_norm[l], w_out=w_out[l],
                  ffn2_norm=ffn2_norm[l], ffn2_w_gate=ffn2_w_gate[l], ffn2_w_up=ffn2_w_up[l],
                  ffn2_w_down=ffn2_w_down[l])
        h0_p = jnp.zeros((xp.shape[0], SSM_GROUPS, SSM_STATE), jnp.complex64)
        xp, k_p, v_p, h_p = decoder_layer(xp, lp, h0_p, None, None)
        h0_s = lax.complex(state_ssm_re[l].astype(jnp.float32), state_ssm_im[l].astype(jnp.float32))
        xs, k_s, v_s, h_s = decoder_layer(xs, lp, h0_s, cache_k[l], cache_v[l])
        pk.append(k_p); pv.append(v_p); pre.append(h_p.real); pim.append(h_p.imag)
        sk.append(k_s); sv.append(v_s); sre.append(h_s.real); sim.append(h_s.imag)
    prompt_k, prompt_v = jnp.stack(pk), jnp.stack(pv)
    prompt_ssm_re, prompt_ssm_im = jnp.stack(pre), jnp.stack(pim)
    sample_k, sample_v = jnp.stack(sk), jnp.stack(sv)
    sample_ssm_re, sample_ssm_im = jnp.stack(sre), jnp.stack(sim)
    return (xp, xs, prompt_k, prompt_v, prompt_ssm_re, prompt_ssm_im,
            sample_k, sample_v, sample_ssm_re, sample_ssm_im)
```

```python
import os
import numpy as np
import concourse.bass as bass
import concourse.mybir as mybir
from concourse.bass_utils import run_bass_kernel_spmd

F32 = mybir.dt.float32
BF16 = mybir.dt.bfloat16
AF = mybir.ActivationFunctionType
ALU = mybir.AluOpType
AX = mybir.AxisListType

D, DFF, DEPTH = 2048, 5632, 2
SEQ, NB, NSEQ_S, LS = 2048, 4, 16, 4
TM, TS = 512, 64
NT = SEQ // TM
T = TM + TS
KC, FC = D // 128, DFF // 128
INW = 2816
ZC = INW // 128
KVH, HD, NH = 4, 64, 16
EPS = 1e-6
NCORES = 8


class Tracker:
    def __init__(self, nc, n_dma_sems=16):
        self.nc = nc
        self.eng = {"pe": nc.tensor, "act": nc.scalar, "dve": nc.vector, "pool": nc.gpsimd, "sp": nc.sync}
        self.sem, self.cnt = {}, {}
        for e in self.eng:
            self.sem[e] = nc.alloc_semaphore(f"s_{e}")
            self.cnt[e] = 0
        self.seen = {e: {} for e in self.eng}
        self.bufs = {}
        self.pending = {e: ([], []) for e in self.eng}
        self.dma_sems = [nc.alloc_semaphore(f"s_dma{i}") for i in range(2 * n_dma_sems)]
        self.dma_val = [0] * (2 * n_dma_sems)
        self.dma_rr = {"pool": 0, "other": 0}
        self.n_dma = n_dma_sems
        self.n_pool = 16
        self.samesync = {"pe": False, "act": True, "dve": True, "pool": True, "sp": True}

    def _wait(self, e, ev):
        if ev is None:
            return
        sem, val, owner = ev
        if owner == e and not self.samesync[e]:
            return
        key = sem.name
        if self.seen[e].get(key, 0) >= val:
            return
        self.eng[e].wait_ge(sem, val)
        self.seen[e][key] = val

    def _deps(self, e, reads, writes, grp=None):
        for k in reads:
            b = self.bufs.get(k)
            if b is not None:
                for w in b[0]:
                    self._wait(e, w)
        for k in writes:
            b = self.bufs.get(k)
            if b is not None:
                if grp is not None and b[2] == grp:
                    continue
                for w in b[0]:
                    self._wait(e, w)
                for r in b[1]:
                    self._wait(e, r)

    def _commit(self, ev, reads, writes, grp=None):
        for k in reads:
            b = self.bufs.setdefault(k, [[], [], None])
            b[1].append(ev)
            if len(b[1]) > 48:
                b[1] = b[1][-48:]
        for k in writes:
            b = self.bufs.get(k)
            if grp is not None and b is not None and b[2] == grp:
                b[0].append(ev)
            else:
                self.bufs[k] = [[ev], [], grp]

    def op(self, e, fn, reads=(), writes=(), inc=True):
        self._deps(e, reads, writes)
        ins = fn(self.eng[e])
        pr, pw = self.pending[e]
        if not inc:
            pr.extend(reads)
            pw.extend(writes)
            return ins
        self.cnt[e] += 1
        ins.then_inc(self.sem[e], 1)
        ev = (self.sem[e], self.cnt[e], e)
        self._commit(ev, list(reads) + pr, list(writes) + pw)
        self.pending[e] = ([], [])
        return ins

    def dma(self, q, out, in_, reads=(), writes=(), grp=None, **kw):
        qk = "pool" if q == "pool" else "other"
        i = self.dma_rr[qk] + (0 if qk == "pool" else self.n_dma)
        self.dma_rr[qk] = (self.dma_rr[qk] + 1) % (self.n_pool if qk == "pool" else self.n_dma)
        sem = self.dma_sems[i]
        if self.dma_val[i] > 0:
            self._wait(q, (sem, self.dma_val[i], "dma"))
        self._deps(q, reads, writes, grp)
        self.dma_val[i] += 16
        self.eng[q].dma_start(out=out, in_=in_, **kw).then_inc(sem, 16)
        ev = (sem, self.dma_val[i], "dma")
        self._commit(ev, list(reads), list(writes), grp)
        return ev

    def barrier(self):
        for e in self.eng:
            for f in self.eng:
                if f != e and self.cnt[f] > 0:
                    self._wait(e, (self.sem[f], self.cnt[f], f))
            for i, sem in enumerate(self.dma_sems):
                if self.dma_val[i] > 0:
                    self._wait(e, (sem, self.dma_val[i], "dma"))
        self.bufs = {}

    def handoff(self, src_keys, dst_keys):
        evs = []
        for k in src_keys:
            b = self.bufs.get(k)
            if b is not None:
                evs.extend(b[0])
                evs.extend(b[1])
        for k in dst_keys:
            self.bufs[k] = [[], list(evs), None]

    def finish(self):
        for i, sem in enumerate(self.dma_sems):
            if self.dma_val[i] > 0:
                self._wait("sp", (sem, self.dma_val[i], "dma"))
        for e in self.eng:
            if e != "sp" and self.cnt[e] > 0:
                self._wait("sp", (self.sem[e], self.cnt[e], e))


def build(NT=NT):
    nc = bass.Bass("TRN2", target_bir_lowering=False)
    dt = lambda n, s, k="ExternalInput": nc.dram_tensor(n, s, F32, kind=k).ap()
    xTp = dt("xTp", [D, SEQ])
    xTs = dt("xTs", [D, TS])
    yTp = dt("yTp", [D, SEQ], "ExternalOutput")
    yTs = dt("yTs", [D, TS], "ExternalOutput")
    W = {}
    for nm, shp in [("ffn1_w_gate", [DEPTH, D, DFF]), ("ffn1_w_up", [DEPTH, D, DFF]), ("ffn1_w_down", [DEPTH, DFF, D]),
                    ("ffn2_w_gate", [DEPTH, D, DFF]), ("ffn2_w_up", [DEPTH, D, DFF]), ("ffn2_w_down", [DEPTH, DFF, D]),
                    ("w_in_aug", [DEPTH, D, INW]), ("w_out", [DEPTH, D, D]),
                    ("ffn1_norm", [DEPTH, D]), ("mix_norm", [DEPTH, D]), ("ffn2_norm", [DEPTH, D]),
                    ("ssm_out_norm", [DEPTH, 1024]), ("attn_out_norm", [DEPTH, 1024])]:
        W[nm] = dt(nm, shp)
    Etab_d = dt("Etab", [128, 2 * 4 * 512])
    Entab_d = dt("Entab", [128, 1024])
    Esamp_d = dt("Esamp", [128, 64])
    kcd_d = dt("kcd", [DEPTH, 128, NSEQ_S * 512])
    vcd_d = dt("vcd", [DEPTH, 128, NSEQ_S * 512])
    ckraw_d = dt("ck_raw", [DEPTH, NSEQ_S, 128, 256])
    cvraw_d = dt("cv_raw", [DEPTH, NSEQ_S, 128, 256])
    PK_D = dt("PK_D", [DEPTH, 128, 256], "ExternalOutput")
    PV_D = dt("PV_D", [DEPTH, 128, 256], "ExternalOutput")
    SK_D = dt("SK_D", [DEPTH, NSEQ_S, 128, 256], "ExternalOutput")
    SV_D = dt("SV_D", [DEPTH, NSEQ_S, 128, 256], "ExternalOutput")
    ident_d = dt("ident", [128, 128])
    bdones_d = dt("bdones", [128, 128])
    gqk_d = dt("gqk", [128, DEPTH * 2])
    sinks_d = dt("sinks_rep", [128, DEPTH * 8])
    ssm_in = {}
    for nm, shp in [("Are", [DEPTH, 128, 64]), ("Aim", [DEPTH, 128, 64]), ("ldt", [DEPTH, 128, 64]),
                    ("Bq1", [DEPTH, 128, 1024]), ("Bq2", [DEPTH, 128, 1024]), ("Cq1", [DEPTH, 128, 1024]), ("Cq2", [DEPTH, 128, 1024]),
                    ("sgn", [128, 2]), ("mask_ts", [128, 128]), ("kvals", [128, 24]), ("jvals", [128, 64])]:
        ssm_in[nm] = dt(nm, shp)
    dbg = bool(int(os.environ.get("DEBUG_SSM", "0")))
    skind = "ExternalOutput" if dbg else "Internal"
    MATS_D = nc.dram_tensor("MATS_D", [DEPTH, 8, 128, 4096], BF16, kind=skind).ap()
    WTAB_D = nc.dram_tensor("WTAB_D", [DEPTH, 8, 128, 1024], F32, kind=skind).ap()
    Dv_d = dt("Dv", [128, DEPTH * 8])
    wglu_d = dt("wglu_bd", [128, DEPTH * 8 * 128])
    h0_d = dt("h0", [DEPTH, 128, 64 * NSEQ_S])
    h0sw_d = dt("h0sw", [DEPTH, 128, 64 * NSEQ_S])
    HS_D = dt("HS_D", [DEPTH, 128, 64 * NSEQ_S], "ExternalOutput")
    HP_D = dt("HP_D", [DEPTH, 128, 64], "ExternalOutput")
    UB_D = nc.dram_tensor("UB_D", [8, 128, 512], BF16, kind="Internal").ap()
    YD = nc.dram_tensor("YD", [8, 128, 512], F32, kind="Internal").ap()
    UBS_D = nc.dram_tensor("UBS_D", [8, 128, 64], BF16, kind="Internal").ap()
    YDS = nc.dram_tensor("YDS", [8, 128, 64], F32, kind="Internal").ap()
    tr = Tracker(nc)
    sb = nc.alloc_sbuf_tensor
    ub = sb("ub", [128, 8, T], BF16)
    Xc = [sb(f"Xc{i}", [128, 8, 64], BF16) for i in range(2)]
    Xs = sb("Xs", [128, 8, 16], BF16)
    hin = [sb(f"hin{L}", [128, 64], F32) for L in range(DEPTH)]
    hinsw = [sb(f"hinsw{L}", [128, 64], F32) for L in range(DEPTH)]
    h0b = sb("h0b", [128, 8, 16], BF16)
    Dv = sb("Dv_s", [128, DEPTH, 8], F32)
    wglu = sb("wglu_s", [128, DEPTH, 8, 128], BF16)
    r8 = sb("r8", [128, DEPTH, 64], F32)
    pw4 = sb("pw4", [128, DEPTH, 2, 64], F32)
    Etab = sb("Etab_s", [128, 2, 4, 512], BF16)
    ident = sb("ident_s", [128, 128], F32)
    bdones = sb("bdones_s", [128, 128], BF16)
    gqk = sb("gqk_s", [128, DEPTH, 2], F32)
    sinkexp = sb("sinkexp_s", [128, DEPTH, 4, 2], F32)
    vtok = sb("vtok", [128, 4, 4, 128], BF16)
    kprev = [sb(f"kprev{L}", [128, 8, 128], BF16) for L in range(DEPTH)]
    vprev = [sb(f"vprev{L}", [128, 4, 128], BF16) for L in range(DEPTH)]
    qzs = sb("qzs", [128, 2, 8, 64], BF16)
    Esamp = sb("Esamp_s", [128, 16, 4], BF16)
    identb = sb("identb", [128, 128], BF16)
    Entab = sb("Entab_s", [128, 2, 512], BF16)
    vsdup = sb("vsdup", [128, 4, 128], BF16)
    pT = [sb(f"pT{i}", [128, 512], BF16) for i in range(2)]
    rec = [sb(f"rec{i}", [128, 256], F32) for i in range(2)]
    x = sb("x", [128, KC, T], F32)
    xn = sb("xn", [128, KC, T], BF16)
    big = sb("big", [128, FC * T // 2], F32)
    h = big[:, :].bitcast(BF16).rearrange("p (f t) -> p f t", f=FC)
    scr = big[:, 0:KC * T].rearrange("p (c t) -> p c t", c=KC)
    z = big[:, 0:ZC * T].rearrange("p (c t) -> p c t", c=ZC)
    gains = {nm: sb("g_" + nm, [128, DEPTH, n // 128], F32) for nm, n in
             [("ffn1_norm", D), ("mix_norm", D), ("ffn2_norm", D), ("ssm_out_norm", 1024), ("attn_out_norm", 1024)]}
    ones_b = sb("ones_b", [128, 128], BF16)
    sqb = [sb(f"sqb{i}", [128, T], BF16) for i in range(2)]
    rstd = sb("rstd", [128, T], F32)
    tmp = [sb(f"tmp{i}", [128, T], F32) for i in range(2)]
    rsb = tmp
    NSLOT = 3
    SLOTW = 8192
    wslot = [sb(f"wslot{i}", [128, SLOTW], BF16) for i in range(NSLOT)]
    bank = [nc.alloc_psum_tensor(f"bank{i}", [128, 512], F32) for i in range(8)]
    B = lambda i: ("B", i)

    with nc.allow_non_contiguous_dma(reason="tiny gain vectors"):
        for nm, g in gains.items():
            for L in range(DEPTH):
                tr.dma("sp", g[:, L, :], W[nm][L].rearrange("(c p) -> p c", p=128), writes=[("gain", nm)])
    tr.op("dve", lambda e: e.memset(ones_b[:], 1.0), writes=["ones"])
    tr.op("dve", lambda e: e.memset(qzs[:], 0.0), writes=["qzs"])
    tr.op("dve", lambda e: e.memset(vsdup[:], 0.0), writes=["vsdup"])
    tr.dma("pool", Entab[:], Entab_d.rearrange("p (k n) -> p k n", k=2), writes=["consts"])
    tr.dma("pool", Etab[:, 0, :, :], Etab_d[:, 0:2048].rearrange("p (k n) -> p k n", k=4), writes=["consts"])
    tr.dma("pool", Etab[:, 1, :, :], Etab_d[:, 2048:4096].rearrange("p (k n) -> p k n", k=4), writes=["consts"])
    tr.dma("pool", bdones[:], bdones_d, writes=["consts"])
    tr.dma("pool", identb[:], ident_d, writes=["consts"])
    tr.dma("pool", Esamp[:], Esamp_d.rearrange("p (h i) -> p h i", i=4), writes=["consts"])
    tr.dma("sp", ident[:], ident_d, writes=["consts"])
    tr.dma("sp", gqk[:], gqk_d.rearrange("p (l i) -> p l i", l=DEPTH), writes=["consts"])
    tr.dma("sp", sinkexp[:], sinks_d.rearrange("p (l k c) -> p l k c", l=DEPTH, k=4), writes=["consts"])
    tr.op("act", lambda e: e.activation(out=sinkexp[:], in_=sinkexp[:], func=AF.Exp), reads=[], writes=["consts"])
    tr.dma("sp", Dv[:], Dv_d.rearrange("p (l c) -> p l c", l=DEPTH), writes=["consts"])
    for L in range(DEPTH):
        tr.dma("pool", wglu[:, L, :, :], wglu_d[:, L * 1024:(L + 1) * 1024].rearrange("p (c m) -> p c m", c=8), writes=["consts"])
        tr.op("dve", lambda e: e.memset(hin[L][:], 0.0), writes=[("hin", L)])
        tr.op("dve", lambda e: e.memset(hinsw[L][:], 0.0), writes=[("hin", L)])
    tr.op("dve", lambda e: e.memset(Xs[:], 0.0), writes=["Xs"])


    TWO_PI = 6.283185307179586
    MAGIC = 12582912.0

    def ssm_setup(L):
        pools = [wslot[0][:, :].bitcast(F32), wslot[1][:, :].bitcast(F32), wslot[2][:, :].bitcast(F32), xn[:, :, :].rearrange("p c t -> p (c t)").bitcast(F32)]
        smalls = [tmp[0], tmp[1], rstd]
        st_ = {"pi": 0, "off": 0, "si": 0, "soff": 0}

        def carve(n):
            while st_["off"] + n > pools[st_["pi"]].shape[1]:
                st_["pi"] += 1
                st_["off"] = 0
            v = pools[st_["pi"]][:, st_["off"]:st_["off"] + n]
            st_["off"] += n
            return v

        def small():
            if st_["soff"] + 64 > T:
                st_["si"] += 1
                st_["soff"] = 0
            v = smalls[st_["si"]][:, st_["soff"]:st_["soff"] + 64]
            st_["soff"] += 64
            return v
        K_ = "setup"
        D_ = lambda fn: tr.op("dve", fn, reads=[], writes=[K_])
        A_ = lambda fn: tr.op("act", fn, reads=[], writes=[K_])
        bigf = big
        xf = x[:, :, :].rearrange("p c t -> p (c t)")

        def sincos(ang, cos_o, sin_o, tA, cycles=False):
            D_(lambda e: e.tensor_scalar(out=tA, in0=ang, scalar1=(1.0 if cycles else 1.0 / TWO_PI), scalar2=None, op0=ALU.mult))
            D_(lambda e: e.tensor_scalar(out=sin_o, in0=tA, scalar1=MAGIC, scalar2=None, op0=ALU.add))
            D_(lambda e: e.tensor_scalar(out=sin_o, in0=sin_o, scalar1=MAGIC, scalar2=None, op0=ALU.subtract))
            D_(lambda e: e.tensor_tensor(out=sin_o, in0=tA, in1=sin_o, op=ALU.subtract))
            D_(lambda e: e.tensor_scalar(out=cos_o, in0=tA, scalar1=0.25, scalar2=None, op0=ALU.add))
            D_(lambda e: e.tensor_scalar(out=tA, in0=cos_o, scalar1=MAGIC, scalar2=None, op0=ALU.add))
            D_(lambda e: e.tensor_scalar(out=tA, in0=tA, scalar1=MAGIC, scalar2=None, op0=ALU.subtract))
            D_(lambda e: e.tensor_tensor(out=cos_o, in0=cos_o, in1=tA, op=ALU.subtract))
            A_(lambda e: e.activation(out=sin_o, in_=sin_o, func=AF.Sin, scale=6.28318))
            A_(lambda e: e.activation(out=cos_o, in_=cos_o, func=AF.Sin, scale=6.28318))

        tAre, tAim, tldt = small(), small(), small()
        sgn, kv, jv = small()[:, 0:2], small()[:, 0:24], small()
        maskt = carve(128)
        Bq1, Bq2, Cq1, Cq2 = carve(1024), carve(1024), carve(1024), carve(1024)
        for dst, nm in [(tAre, "Are"), (tAim, "Aim"), (tldt, "ldt")]:
            tr.dma("sp", dst, ssm_in[nm][L], writes=[K_], grp=("setup", L))
        for dst, nm in [(sgn, "sgn"), (kv, "kvals"), (jv, "jvals"), (maskt, "mask_ts")]:
            tr.dma("sp", dst, ssm_in[nm], writes=[K_], grp=("setup", L))
        for dst, nm in [(Bq1, "Bq1"), (Bq2, "Bq2"), (Cq1, "Cq1"), (Cq2, "Cq2")]:
            tr.dma("sp", dst, ssm_in[nm][L], writes=[K_], grp=("setup", L))
        dtt, ar, th, e1, c1, s1, lbr, lbi, nr, den, cr, ci, sci, t64, nsgn = [small() for _ in range(15)]
        nsgn = nsgn[:, 0:1]
        A_(lambda e: e.activation(out=dtt, in_=tldt, func=AF.Exp))
        D_(lambda e: e.tensor_tensor(out=ar, in0=tAre, in1=dtt, op=ALU.mult))
        D_(lambda e: e.tensor_tensor(out=th, in0=tAim, in1=dtt, op=ALU.mult))
        A_(lambda e: e.activation(out=e1, in_=ar, func=AF.Exp))
        A_(lambda e: e.activation(out=r8[:, L, :], in_=ar, func=AF.Exp, scale=8.0))
        sincos(th, c1, s1, t64)
        D_(lambda e: e.tensor_tensor(out=lbr, in0=e1, in1=c1, op=ALU.mult))
        D_(lambda e: e.tensor_tensor(out=lbi, in0=e1, in1=s1, op=ALU.mult))
        D_(lambda e: e.tensor_scalar(out=nr, in0=lbr, scalar1=-1.0, scalar2=None, op0=ALU.add))
        D_(lambda e: e.tensor_tensor(out=den, in0=tAre, in1=tAre, op=ALU.mult))
        D_(lambda e: e.tensor_tensor(out=t64, in0=tAim, in1=tAim, op=ALU.mult))
        D_(lambda e: e.tensor_tensor(out=den, in0=den, in1=t64, op=ALU.add))
        D_(lambda e: e.reciprocal(out=den, in_=den))
        D_(lambda e: e.tensor_tensor(out=cr, in0=nr, in1=tAre, op=ALU.mult))
        D_(lambda e: e.tensor_tensor(out=t64, in0=lbi, in1=tAim, op=ALU.mult))
        D_(lambda e: e.tensor_tensor(out=cr, in0=cr, in1=t64, op=ALU.add))
        D_(lambda e: e.tensor_tensor(out=cr, in0=cr, in1=den, op=ALU.mult))
        D_(lambda e: e.tensor_tensor(out=ci, in0=lbi, in1=tAre, op=ALU.mult))
        D_(lambda e: e.tensor_tensor(out=t64, in0=nr, in1=tAim, op=ALU.mult))
        D_(lambda e: e.tensor_tensor(out=ci, in0=ci, in1=t64, op=ALU.subtract))
        D_(lambda e: e.tensor_tensor(out=ci, in0=ci, in1=den, op=ALU.mult))
        D_(lambda e: e.tensor_scalar(out=sci, in0=ci, scalar1=sgn[:, 0:1], scalar2=None, op0=ALU.mult))
        D_(lambda e: e.tensor_scalar(out=nsgn, in0=sgn[:, 0:1], scalar1=-1.0, scalar2=None, op0=ALU.mult))
        th8f, t64b = small(), small()
        D_(lambda e: e.tensor_scalar(out=th8f, in0=th, scalar1=8.0 / TWO_PI, scalar2=None, op0=ALU.mult))
        D_(lambda e: e.tensor_scalar(out=t64b, in0=th8f, scalar1=MAGIC, scalar2=None, op0=ALU.add))
        D_(lambda e: e.tensor_scalar(out=t64b, in0=t64b, scalar1=MAGIC, scalar2=None, op0=ALU.subtract))
        D_(lambda e: e.tensor_tensor(out=th8f, in0=th8f, in1=t64b, op=ALU.subtract))
        BQ1, BQ2, tB = carve(1024), carve(1024), carve(1024)
        v3 = lambda a: a.rearrange("p (g c) -> p g c", c=16)
        bc = lambda a: a.unsqueeze(2).broadcast_to([128, 64, 16])
        D_(lambda e: e.tensor_tensor(out=v3(BQ1), in0=v3(Bq1), in1=bc(cr), op=ALU.mult))
        D_(lambda e: e.tensor_tensor(out=v3(tB), in0=v3(Bq2), in1=bc(sci), op=ALU.mult))
        D_(lambda e: e.tensor_tensor(out=BQ1, in0=BQ1, in1=tB, op=ALU.add))
        D_(lambda e: e.tensor_tensor(out=v3(BQ2), in0=v3(Bq2), in1=bc(cr), op=ALU.mult))
        D_(lambda e: e.tensor_tensor(out=v3(tB), in0=v3(Bq1), in1=bc(sci), op=ALU.mult))
        D_(lambda e: e.tensor_tensor(out=BQ2, in0=BQ2, in1=tB, op=ALU.subtract))
        D_(lambda e: e.tensor_scalar(out=Cq1[64:128, :], in0=Cq1[64:128, :], scalar1=-1.0, scalar2=None, op0=ALU.mult))
        D_(lambda e: e.tensor_scalar(out=Cq2[64:128, :], in0=Cq2[64:128, :], scalar1=-1.0, scalar2=None, op0=ALU.mult))
        P1, P2 = carve(1536), carve(1536)
        ang, tA, ek, ck, sk = [bigf[:, i * 1536:(i + 1) * 1536] for i in range(5)]
        k3 = lambda a: a.rearrange("p (k g) -> p k g", g=64)
        kb_ = kv.unsqueeze(2).broadcast_to([128, 24, 64])
        gb_ = lambda a: a.unsqueeze(1).broadcast_to([128, 24, 64])
        D_(lambda e: e.tensor_tensor(out=k3(ang), in0=kb_, in1=gb_(th), op=ALU.mult))
        D_(lambda e: e.tensor_tensor(out=k3(ek), in0=kb_, in1=gb_(ar), op=ALU.mult))
        A_(lambda e: e.activation(out=ek, in_=ek, func=AF.Exp))
        sincos(ang, ck, sk, tA)
        D_(lambda e: e.tensor_tensor(out=P1, in0=ek, in1=ck, op=ALU.mult))
        D_(lambda e: e.tensor_tensor(out=P2, in0=ek, in1=sk, op=ALU.mult))
        D_(lambda e: e.tensor_scalar(out=P2, in0=P2, scalar1=sgn[:, 0:1], scalar2=None, op0=ALU.mult))
        D_(lambda e: e.tensor_copy(out=pw4[:, L, 0, :], in_=k3(P1)[:, 3, :]))
        D_(lambda e: e.tensor_copy(out=pw4[:, L, 1, :], in_=k3(P2)[:, 3, :]))
        stage = carve(2048).bitcast(BF16).rearrange("p (k g m) -> p k g m", k=4, g=8)
        wstage = carve(2048).rearrange("p (w g j) -> p w g j", w=2, g=16)
        P1k, P2k = k3(P1), k3(P2)
        for half in range(2):
            g0 = 32 * half
            Cst = xf[:, 0:4608].rearrange("p (g t c) -> p g t c", g=32, t=9)
            tC = bigf[:, 0:4608].rearrange("p (g t c) -> p g t c", g=32, t=9)
            pC = lambda P: P[:, 15:24, g0:g0 + 32].rearrange("p t g -> p g t").unsqueeze(3).broadcast_to([128, 32, 9, 16])
            qC = lambda Q: v3(Q)[:, g0:g0 + 32, :].unsqueeze(2).broadcast_to([128, 32, 9, 16])
            D_(lambda e: e.tensor_tensor(out=Cst, in0=pC(P1k), in1=qC(Cq1), op=ALU.mult))
            D_(lambda e: e.tensor_tensor(out=tC, in0=pC(P2k), in1=qC(Cq2), op=ALU.mult))
            D_(lambda e: e.tensor_tensor(out=Cst, in0=Cst, in1=tC, op=ALU.add))
            Bst = bigf[:, 0:4096].rearrange("p (g s c) -> p g s c", g=32, s=8)
            Sst = bigf[:, 4096:8192].rearrange("p (g s c) -> p g s c", g=32, s=8)
            tS = bigf[:, 8192:12288].rearrange("p (g s c) -> p g s c", g=32, s=8)
            pB = lambda P, i0: P[:, i0:i0 + 8, g0:g0 + 32].rearrange("p s g -> p g s").unsqueeze(3).broadcast_to([128, 32, 8, 16])
            qB = lambda Q: v3(Q)[:, g0:g0 + 32, :].unsqueeze(2).broadcast_to([128, 32, 8, 16])
            for dst, i0 in ((Bst, 7), (Sst, 0)):
                D_(lambda e: e.tensor_tensor(out=dst, in0=pB(P1k, i0), in1=qB(BQ1), op=ALU.mult))
                D_(lambda e: e.tensor_tensor(out=tS, in0=pB(P2k, i0), in1=qB(BQ2), op=ALU.mult))
                D_(lambda e: e.tensor_tensor(out=dst, in0=dst, in1=tS, op=ALU.add))
            for cq in range(4):
                chunk = 4 * half + cq
                for g4 in range(2):
                    for i in range(4):
                        gl = 8 * cq + 4 * g4 + i
                        tr.op("pe", lambda e: e.matmul(bank[0][:, i * 128:(i + 1) * 128],
                                                       lhsT=Bst[:, gl, :, :].rearrange("p s c -> p (s c)"),
                                                       rhs=Cst[:, gl, 0:8, :].rearrange("p t c -> p (t c)"), start=True, stop=True),
                              reads=[K_], writes=[B(0)], inc=(i == 3))
                        tr.op("pe", lambda e: e.transpose(out=bank[1][:, i * 128:(i + 1) * 128],
                                                          in_=Sst[:, gl, :, :].rearrange("p s c -> p (s c)"), identity=ident[:]),
                              reads=[K_, "consts"], writes=[B(1)], inc=(i == 3))
                    gs = slice(4 * g4, 4 * g4 + 4)
                    b0v = bank[0][:, :].rearrange("p (g m) -> p g m", g=4)
                    b1v = bank[1][:, :].rearrange("p (g m) -> p g m", g=4)
                    tr.op("dve", lambda e: e.tensor_tensor(out=stage[:, 0, gs, :], in0=b0v,
                                                           in1=maskt.unsqueeze(1).broadcast_to([128, 4, 128]), op=ALU.mult),
                          reads=[], writes=[B(0), K_])
                    tr.op("act", lambda e: e.activation(out=stage[:, 2, gs, :], in_=b1v, func=AF.Copy), reads=[], writes=[B(1), K_])
                    tr.op("act", lambda e: e.activation(out=stage[:, 3, gs, 0:64], in_=b1v[:, :, 64:128], func=AF.Copy),
                          reads=[], writes=[B(1), K_])
                    tr.op("act", lambda e: e.activation(out=stage[:, 3, gs, 64:128], in_=b1v[:, :, 0:64], func=AF.Copy),
                          reads=[], writes=[B(1), K_])
                D_(lambda e: e.tensor_copy(out=stage[:, 1, :, :],
                                           in_=Cst[:, 8 * cq:8 * cq + 8, 1:9, :].rearrange("p g t c -> p g (t c)")))
                tr.dma("sp", MATS_D[L, chunk], stage[:, :, :, :].rearrange("p k g m -> p (k g m)"), reads=[K_])
            for qtr in range(2):
                gq0 = g0 + 16 * qtr
                angW, cosW, sinW, tAW = [xf[:, 4608 + i * 1024:4608 + (i + 1) * 1024] for i in range(4)]
                w3 = lambda a: a.rearrange("p (g j) -> p g j", j=64)
                D_(lambda e: e.tensor_tensor(out=w3(angW), in0=jv.unsqueeze(1).broadcast_to([128, 16, 64]),
                                             in1=th8f[:, gq0:gq0 + 16].unsqueeze(2).broadcast_to([128, 16, 64]), op=ALU.mult))
                sincos(angW, cosW, sinW, tAW, cycles=True)
                D_(lambda e: e.tensor_copy(out=wstage[:, 0, :, :], in_=w3(cosW)))
                D_(lambda e: e.tensor_scalar(out=wstage[:, 1, :, :], in0=w3(sinW), scalar1=nsgn, scalar2=None, op0=ALU.mult))
                for c2 in range(2):
                    chunk = gq0 // 8 + c2
                    tr.dma("sp", WTAB_D[L, chunk].rearrange("p (w g j) -> p w g j", w=2, g=8),
                           wstage[:, :, 8 * c2:8 * c2 + 8, :], reads=[K_])

    for L in range(DEPTH):
        ssm_setup(L)
    tr.barrier()

    plan = []

    wcol_of = lambda kcin: 256 if kcin * 256 <= SLOTW else 128

    def plan_linear(wd, kcin, nout):
        wv = wd.rearrange("(c p) n -> p c n", p=128)
        wc = wcol_of(kcin)
        for m2 in range(nout // wc):
            def mk(slot, wv=wv, m2=m2, kcin=kcin, wc=wc):
                v = wslot[slot][:, 0:kcin * wc].rearrange("p (c n) -> p c n", c=kcin)
                return [(v[:, c0:min(c0 + 4, kcin), :], wv[:, c0:min(c0 + 4, kcin), m2 * wc:(m2 + 1) * wc], None)
                        for c0 in range(0, kcin, 4)]
            plan.append(mk)

    def plan_gu(wg, wu):
        gv = wg.rearrange("(c p) n -> p c n", p=128)
        uv = wu.rearrange("(c p) n -> p c n", p=128)
        for m2 in range(DFF // 256):
            def mk(slot, gv=gv, uv=uv, m2=m2):
                v = wslot[slot][:, 0:2 * KC * 256].rearrange("p (g c n) -> p g c n", g=2, c=KC)
                out = []
                for gi, sv in enumerate((gv, uv)):
                    for c0 in range(0, KC, 4):
                        out.append((v[:, gi, c0:c0 + 4, :], sv[:, c0:c0 + 4, m2 * 256:(m2 + 1) * 256], gi))
                return out
            plan.append(mk)

    for t in range(NT):
        for L in range(DEPTH):
            plan_gu(W["ffn1_w_gate"][L], W["ffn1_w_up"][L])
            plan_linear(W["ffn1_w_down"][L], FC, D)
            plan_linear(W["w_in_aug"][L], KC, INW)
            plan_linear(W["w_out"][L], KC, D)
            plan_gu(W["ffn2_w_gate"][L], W["ffn2_w_up"][L])
            plan_linear(W["ffn2_w_down"][L], FC, D)
    st = {"issued": 0, "used": 0}

    def ensure(n):
        while st["issued"] < min(n, len(plan)):
            i = st["issued"]
            slot = i % NSLOT
            for dst, src, gi in plan[i](slot):
                if gi is None:
                    keys = [("ws", slot), ("ws", slot, 0), ("ws", slot, 1)]
                else:
                    keys = [("ws", slot, gi)] + ([("ws", slot)] if gi == 0 else [])
                tr.dma("pool", dst, src, writes=keys, grp=("ws", i))
            st["issued"] += 1

    def next_slot():
        i = st["used"]
        st["used"] += 1
        ensure(i + NSLOT)
        return i % NSLOT

    def rmsnorm(src, nch, gname, L, dst, S, n_feat, src_keys, coff=0, doff=0, perm=False):
        W_ = TM + S
        for c in range(nch):
            sq = sqb[c % 2]
            last = (c == nch - 1)
            tr.op("act", lambda e: e.activation(out=sq[:, 0:W_], in_=src[:, coff + c, 0:W_], func=AF.Square),
                  reads=src_keys, writes=[("sqb", c % 2)])
            tr.op("pe", lambda e: e.matmul(bank[6][:, 0:TM], lhsT=ones_b[:], rhs=sq[:, 0:TM], start=(c == 0), stop=last),
                  reads=["ones", ("sqb", c % 2)], writes=[B(6)], inc=(not S))
            if S:
                tr.op("pe", lambda e: e.matmul(bank[7][:, 0:S], lhsT=ones_b[:], rhs=sq[:, TM:TM + S], start=(c == 0), stop=last),
                      reads=[], writes=[B(7)], inc=True)
        tr.op("act", lambda e: e.activation(out=rstd[:, 0:TM], in_=bank[6][:, 0:TM], func=AF.Sqrt, scale=1.0 / n_feat, bias=EPS),
              reads=[], writes=[B(6), "rstd"])
        if S:
            tr.op("act", lambda e: e.activation(out=rstd[:, TM:TM + S], in_=bank[7][:, 0:S], func=AF.Sqrt, scale=1.0 / n_feat, bias=EPS),
                  reads=[], writes=[B(7), "rstd"])
        tr.op("dve", lambda e: e.reciprocal(out=rstd[:, 0:W_], in_=rstd[:, 0:W_]), reads=[], writes=["rstd"])
        for c in range(nch):
            if perm:
                pv = lambda a: a.rearrange("p (s j) -> p s j", s=8)
                tr.op("dve", lambda e: e.scalar_tensor_tensor(out=dst[:, doff + c, 0:TM].rearrange("p (j s) -> p s j", s=8),
                                                              in0=pv(src[:, coff + c, 0:TM]), scalar=gains[gname][:, L, c:c + 1],
                                                              in1=pv(rstd[:, 0:TM]), op0=ALU.mult, op1=ALU.mult),
                      reads=list(src_keys) + ["rstd", ("gain", gname)], writes=[("xn", doff + c)])
                if S:
                    tr.op("dve", lambda e: e.scalar_tensor_tensor(out=dst[:, doff + c, TM:W_], in0=src[:, coff + c, TM:W_],
                                                                  scalar=gains[gname][:, L, c:c + 1],
                                                                  in1=rstd[:, TM:W_], op0=ALU.mult, op1=ALU.mult),
                          reads=list(src_keys) + ["rstd", ("gain", gname)], writes=[("xn", doff + c)])
                continue
            tr.op("dve", lambda e: e.scalar_tensor_tensor(out=dst[:, doff + c, 0:W_], in0=src[:, coff + c, 0:W_],
                                                          scalar=gains[gname][:, L, c:c + 1],
                                                          in1=rstd[:, 0:W_], op0=ALU.mult, op1=ALU.mult),
                  reads=list(src_keys) + ["rstd", ("gain", gname)], writes=[("xn", doff + c)])

    def linear(src, kcin, src_keys, nout, S, evac, wcols=None):
        wc = wcol_of(kcin)
        nj = wc // 128
        for m2 in range(nout // wc):
            slot = next_slot()
            wv = wslot[slot][:, 0:kcin * wc].rearrange("p (c n) -> p c n", c=kcin)
            for j in range(nj):
                m = nj * m2 + j
                pb = m % 2
                for k in range(kcin):
                    last = (k == kcin - 1)
                    lw = wv[:, k, j * 128:(j + 1) * 128]
                    tr.op("pe", lambda e: e.matmul(bank[pb][:], lhsT=lw, rhs=src[:, k, 0:TM], start=(k == 0), stop=last),
                          reads=(list(src_keys) + [("ws", slot)]) if k == 0 else [], writes=[B(pb)], inc=(last and not S))
                    if S:
                        tr.op("pe", lambda e: e.matmul(bank[4 + pb][:, 0:S], lhsT=lw, rhs=src[:, k, TM:TM + S], start=(k == 0), stop=last),
                              reads=[], writes=[B(4 + pb)], inc=last)
                evac(m, pb)

    def ffn(L, which, S):
        pre = f"ffn{which}"
        rmsnorm(x, KC, pre + "_norm", L, xn, S, D, ["x"])
        xn_keys = [("xn", c) for c in range(KC)]
        W_ = TM + S
        for m2 in range(DFF // 256):
            slot = next_slot()
            wv = wslot[slot][:, 0:2 * KC * 256].rearrange("p (g c n) -> p g c n", g=2, c=KC)
            for j in range(2):
                f = 2 * m2 + j
                pb = f % 2
                for gi in range(2):
                    bk = 2 * gi + pb
                    for k in range(KC):
                        last = (k == KC - 1)
                        lw = wv[:, gi, k, j * 128:(j + 1) * 128]
                        tr.op("pe", lambda e: e.matmul(bank[bk][:], lhsT=lw, rhs=xn[:, k, 0:TM], start=(k == 0), stop=last),
                              reads=(xn_keys + [("ws", slot, gi)]) if k == 0 else [], writes=[B(bk)], inc=(last and not S))
                        if S:
                            tr.op("pe", lambda e: e.matmul(bank[4 + pb][:, gi * 64:gi * 64 + S], lhsT=lw, rhs=xn[:, k, TM:TM + S],
                                                           start=(k == 0), stop=last), reads=[], writes=[B(4 + pb)], inc=last)
                tr.op("act", lambda e: e.activation(out=tmp[pb][:, 0:TM], in_=bank[pb][:], func=AF.Silu),
                      reads=[], writes=[B(pb), ("tmp", pb)])
                tr.op("dve", lambda e: e.tensor_tensor(out=h[:, f, 0:TM], in0=tmp[pb][:, 0:TM], in1=bank[2 + pb][:], op=ALU.mult),
                      reads=[("tmp", pb)], writes=[B(2 + pb), ("h", f)])
                if S:
                    tr.op("act", lambda e: e.activation(out=tmp[pb][:, TM:TM + S], in_=bank[4 + pb][:, 0:S], func=AF.Silu),
                          reads=[], writes=[B(4 + pb), ("tmps", pb)])
                    tr.op("dve", lambda e: e.tensor_tensor(out=h[:, f, TM:TM + S], in0=tmp[pb][:, TM:TM + S], in1=bank[4 + pb][:, 64:64 + S],
                                                           op=ALU.mult), reads=[("tmps", pb)], writes=[B(4 + pb), ("h", f)])
        h_keys = [("h", f) for f in range(FC)]

        def evac(d, pb):
            tr.op("dve", lambda e: e.scalar_tensor_tensor(out=x[:, d, 0:TM], in0=bank[pb][:], scalar=0.5, in1=x[:, d, 0:TM],
                                                          op0=ALU.mult, op1=ALU.add), reads=[], writes=[B(pb), "x"])
            if S:
                tr.op("dve", lambda e: e.scalar_tensor_tensor(out=x[:, d, TM:TM + S], in0=bank[4 + pb][:, 0:S], scalar=0.5,
                                                              in1=x[:, d, TM:TM + S], op0=ALU.mult, op1=ALU.add),
                      reads=[], writes=[B(4 + pb), "x"])
        linear(h, FC, h_keys, D, S, evac)

    def attention(L, t, S):
        W_ = TM + S
        last_tile = bool(S)
        zf = big
        kcq = zf[:, 0:1024].bitcast(BF16).rearrange("p (b k s) -> p b k s", b=4, k=4)
        vcq = zf[:, 1024:2048].bitcast(BF16).rearrange("p (b k d) -> p b k d", b=4, k=4)
        knf = zf[:, 2048:2816].rearrange("p (k t) -> p k t", k=4)
        ZU = [("z", c) for c in range(8)]
        AK = ["A_kc", "A_vc", "knf"]
        if last_tile:
            tr.handoff(ZU, AK)
            for q4 in range(4):
                bs_ = slice(4 * q4, 4 * q4 + 4)
                tr.dma("sp", SK_D[L, bs_, 0:124, :], ckraw_d[L, bs_, 4:128, :])
                tr.dma("sp", SV_D[L, bs_, 0:124, :], cvraw_d[L, bs_, 4:128, :])
        for c in range(12):
            isq = c < 8
            sc = 8 + c
            bm, bs = (6, 7) if c % 2 == 0 else (4, 5)
            sq, rs = sqb[c % 2], rsb[c % 2]
            tr.op("act", lambda e: e.activation(out=sq[:, 0:W_], in_=z[:, sc, 0:W_], func=AF.Square),
                  reads=[("z", sc)], writes=[("sqb", c % 2)])
            tr.op("pe", lambda e: e.matmul(bank[bm][:, 0:TM], lhsT=bdones[:], rhs=sq[:, 0:TM], start=True, stop=True),
                  reads=[("sqb", c % 2), "consts"], writes=[B(bm)])
            scl, bia = (1.0, 64 * EPS) if isq else (1.0 / 64, EPS)
            tr.op("act", lambda e: e.activation(out=rs[:, 0:TM], in_=bank[bm][:, 0:TM], func=AF.Sqrt, scale=scl, bias=bia),
                  reads=[], writes=[B(bm), ("tmp", c % 2)])
            if S:
                tr.op("pe", lambda e: e.matmul(bank[bs][:, 0:S], lhsT=bdones[:], rhs=sq[:, TM:TM + S], start=True, stop=True),
                      reads=[("sqb", c % 2), "consts"], writes=[B(bs)])
                tr.op("act", lambda e: e.activation(out=rs[:, TM:TM + S], in_=bank[bs][:, 0:S], func=AF.Sqrt, scale=scl, bias=bia),
                      reads=[], writes=[B(bs), ("tmp", c % 2)])
            tr.op("dve", lambda e: e.reciprocal(out=rs[:, 0:W_], in_=rs[:, 0:W_]), reads=[], writes=[("tmp", c % 2)])
            if isq:
                tr.op("dve", lambda e: e.scalar_tensor_tensor(out=xn[:, c, 0:W_], in0=z[:, sc, 0:W_], scalar=gqk[:, L, 0:1],
                                                              in1=rs[:, 0:W_], op0=ALU.mult, op1=ALU.mult),
                      reads=[("z", sc), ("tmp", c % 2), "consts"], writes=[("xn", c)])
            else:
                kvh_ = c - 8
                for par in range(2):
                    lo, hi = 64 * par, 64 * par + 64
                    oc = 8 + 2 * kvh_ + par
                    tr.op("dve", lambda e: e.memset(xn[64 - lo:128 - lo, oc, 0:W_], 0.0), reads=[], writes=[("xn", oc)])
                    tr.op("dve", lambda e: e.scalar_tensor_tensor(out=xn[lo:hi, oc, 0:W_], in0=z[lo:hi, sc, 0:W_], scalar=gqk[lo:hi, L, 1:2],
                                                                  in1=rs[lo:hi, 0:W_], op0=ALU.mult, op1=ALU.mult),
                          reads=[("z", sc), ("tmp", c % 2), "consts"], writes=[("xn", oc)])
                if last_tile:
                    tr.op("dve", lambda e: e.scalar_tensor_tensor(out=knf[:, kvh_, :], in0=z[:, sc, 384:576], scalar=gqk[:, L, 1:2],
                                                                  in1=rs[:, 384:576], op0=ALU.mult, op1=ALU.mult),
                          reads=[("z", sc), ("tmp", c % 2), "consts"], writes=["knf"])
        ATT = int(os.environ.get('ATT_STAGE', '9'))
        if ATT < 2:
            return
        for b in range(4):
            for vc in range(2):
                tr.op("pe", lambda e: e.transpose(out=bank[6][:, vc * 128:(vc + 1) * 128], in_=z[:, 20 + vc, b * 128:(b + 1) * 128],
                                                  identity=ident[:]),
                      reads=[("z", 20 + vc), "consts"], writes=[B(6)], inc=(vc == 1))
            vt3 = bank[6][:, 0:256].rearrange("p (k d) -> p k d", k=4)
            tr.op("act", lambda e: e.activation(out=vtok[:, b, :, 0:64], in_=vt3, func=AF.Copy), reads=[], writes=[B(6), ("vtok", b)])
            tr.op("act", lambda e: e.activation(out=vtok[:, b, :, 64:128], in_=vt3, func=AF.Copy), reads=[], writes=[B(6), ("vtok", b)])
            if last_tile and b == 3:
                tr.op("act", lambda e: e.activation(out=rec[1][:], in_=bank[6][:, 0:256], func=AF.Copy), reads=[], writes=[B(6), ("rec", 1)])
                tr.dma("sp", PV_D[L], rec[1][:], reads=[("rec", 1)])
        if last_tile:
            for kvh in range(KVH):
                tr.op("pe", lambda e: e.transpose(out=bank[6][:, kvh * 128:(kvh + 1) * 128], in_=knf[:, kvh, 0:128], identity=ident[:]),
                      reads=["knf", "consts"], writes=[B(6)], inc=(kvh == 3))
            tr.op("act", lambda e: e.activation(out=rec[0][:].rearrange("p (k d) -> p k d", k=4),
                                                in_=bank[6][:].rearrange("p (k d) -> p k d", k=4)[:, :, 0:64], func=AF.Copy),
                  reads=[], writes=[B(6), ("rec", 0)])
            tr.dma("sp", PK_D[L], rec[0][:], reads=[("rec", 0)])
        items = []
        for b in range(4):
            for kvh in range(KVH):
                kcur = [("xn", 8 + 2 * kvh), ("xn", 9 + 2 * kvh)]
                kbs = []
                if b > 0:
                    kbs.append((lambda par, b=b, kvh=kvh: xn[:, 8 + 2 * kvh + par, (b - 1) * 128:b * 128],
                                vtok[:, b - 1, kvh, :], kcur, [("vtok", b - 1)], 0))
                elif t > 0:
                    kbs.append((lambda par, kvh=kvh: kprev[L][:, 2 * kvh + par, :], vprev[L][:, kvh, :], [("kprev", L)], [("vprev", L)], 0))
                kbs.append((lambda par, b=b, kvh=kvh: xn[:, 8 + 2 * kvh + par, b * 128:(b + 1) * 128],
                            vtok[:, b, kvh, :], kcur, [("vtok", b)], 1))
                for ki, kb in enumerate(kbs):
                    items.append((b, kvh, ki, len(kbs), kb, b * KVH + kvh))

        def emit_scores(idx):
            b, kvh, ki, nk, (kfn, vap, kkeys, vkeys, ei), grp_ = items[idx]
            sbk = idx % 4
            for par in range(2):
                tr.op("pe", lambda e: e.matmul(bank[sbk][:, par * 256:(par + 1) * 256], lhsT=kfn(par),
                                               rhs=xn[:, 2 * kvh:2 * kvh + 2, b * 128:(b + 1) * 128], start=True, stop=False),
                      reads=kkeys + [("xn", 2 * kvh), ("xn", 2 * kvh + 1)], writes=[B(sbk)], inc=False)
                tr.op("pe", lambda e: e.matmul(bank[sbk][:, par * 256:(par + 1) * 256], lhsT=identb[:],
                                               rhs=Etab[:, ei, kvh, par * 256:(par + 1) * 256], start=False, stop=True),
                      reads=["consts"], writes=[B(sbk)], inc=(par == 1))

        def emit_rest(idx):
            b, kvh, ki, nk, (kfn, vap, kkeys, vkeys, ei), grp_ = items[idx]
            q0 = b * 128
            sbk = idx % 4
            pt, pk = pT[idx % 2], ("pT", idx % 2)
            ob, db, ri = 4 + 2 * (grp_ % 2), 5 + 2 * (grp_ % 2), grp_ % 2
            first, lastk = (ki == 0), (ki == nk - 1)
            tr.op("act", lambda e: e.activation(out=pt[:], in_=bank[sbk][:], func=AF.Exp), reads=[], writes=[B(sbk), pk])
            tr.op("pe", lambda e: e.matmul(bank[ob][:], lhsT=vap, rhs=pt[:], start=first, stop=lastk),
                  reads=vkeys + [pk], writes=[B(ob)], inc=False)
            tr.op("pe", lambda e: e.matmul(bank[db][:], lhsT=ones_b[:], rhs=pt[:], start=first, stop=lastk),
                  reads=["ones"], writes=[B(db)], inc=True)
            if not lastk:
                return
            for par in range(2):
                rows = slice(64 * par, 64 * par + 64)
                for c2 in range(2):
                    tr.op("dve", lambda e: e.tensor_scalar(out=rec[ri][rows, c2 * 128:(c2 + 1) * 128],
                                                           in0=bank[db][rows, par * 256 + c2 * 128:par * 256 + (c2 + 1) * 128],
                                                           scalar1=sinkexp[rows, L, kvh, c2:c2 + 1], scalar2=None, op0=ALU.add),
                          reads=["consts"], writes=[B(db), ("rec", ri)])
            tr.op("dve", lambda e: e.reciprocal(out=rec[ri][:], in_=rec[ri][:]), reads=[], writes=[("rec", ri)])
            for par in range(2):
                rows = slice(64 * par, 64 * par + 64)
                tr.op("dve", lambda e: e.tensor_tensor(out=z[rows, 8 + 2 * kvh:10 + 2 * kvh, q0:q0 + 128],
                                                       in0=bank[ob][rows, par * 256:(par + 1) * 256].rearrange("p (c q) -> p c q", c=2),
                                                       in1=rec[ri][rows, :].rearrange("p (c q) -> p c q", c=2), op=ALU.mult),
                      reads=[("rec", ri)], writes=[B(ob), ("z", 8 + 2 * kvh), ("z", 9 + 2 * kvh)])

        emit_scores(0)
        for idx in range(len(items)):
            if idx + 1 < len(items):
                emit_scores(idx + 1)
            emit_rest(idx)
        tr.op("dve", lambda e: e.tensor_copy(out=kprev[L][:], in_=xn[:, 8:16, 384:512]),
              reads=[("xn", 8 + k) for k in range(8)], writes=[("kprev", L)])
        tr.op("dve", lambda e: e.tensor_copy(out=vprev[L][:], in_=vtok[:, 3, :, :]), reads=[("vtok", 3)], writes=[("vprev", L)])
        if not last_tile:
            return
        SC = slice(TM, TM + S)
        XQ = [("xn", c) for c in range(8)]
        XKZ = [("xn", 8 + c) for c in range(8)]
        for kvh in range(KVH):
            tr.op("pe", lambda e: e.transpose(out=bank[6][:, kvh * 128:(kvh + 1) * 128], in_=knf[:, kvh, 64:192], identity=ident[:]),
                  reads=["knf", "consts"], writes=[B(6)], inc=(kvh == 3))
        tr.op("act", lambda e: e.activation(out=rec[0][64:128, :].rearrange("p (k d) -> p k d", k=4),
                                            in_=bank[6][64:128, :].rearrange("p (k d) -> p k d", k=4)[:, :, 0:64], func=AF.Copy),
              reads=[], writes=[B(6), ("rec", 0)])
        for i_ in range(LS):
            tr.dma("sp", SK_D[L, :, 124 + i_, :], rec[0][64 + 16 * i_:64 + 16 * i_ + 16, :], reads=[("rec", 0)])
        for vc in range(2):
            tr.op("pe", lambda e: e.transpose(out=bank[7][:, vc * 128:(vc + 1) * 128], in_=z[:, 20 + vc, 448:576], identity=ident[:]),
                  reads=[("z", 20 + vc), "consts"], writes=[B(7)], inc=(vc == 1))
        vs3 = bank[7][64:128, 0:256].rearrange("p (k d) -> p k d", k=4)
        tr.op("act", lambda e: e.activation(out=vsdup[64:128, :, 0:64], in_=vs3, func=AF.Copy), reads=[], writes=[B(7), "vsdup"])
        tr.op("act", lambda e: e.activation(out=vsdup[64:128, :, 64:128], in_=vs3, func=AF.Copy), reads=[], writes=[B(7), "vsdup"])
        tr.op("act", lambda e: e.activation(out=rec[1][64:128, :], in_=bank[7][64:128, 0:256], func=AF.Copy), reads=[], writes=[B(7), ("rec", 1)])
        for i_ in range(LS):
            tr.dma("sp", SV_D[L, :, 124 + i_, :], rec[1][64 + 16 * i_:64 + 16 * i_ + 16, :], reads=[("rec", 1)])
        for par in range(2):
            rows = slice(64 * par, 64 * par + 64)
            tr.op("dve", lambda e: e.tensor_copy(out=qzs[rows, par, :, :], in_=xn[rows, 0:8, SC]), reads=XQ, writes=["qzs"])
        for kvh in range(KVH):
            for par in range(2):
                o_ = ((kvh % 2) * 2 + par) * 128
                tr.op("pe", lambda e: e.matmul(bank[kvh // 2][:, o_:o_ + 128], lhsT=xn[:, 8 + 2 * kvh + par, 448:576],
                                               rhs=xn[:, 2 * kvh:2 * kvh + 2, SC], start=True, stop=True),
                      reads=XQ + XKZ, writes=[B(kvh // 2)])
        for hb in range(2):
            tr.op("act", lambda e: e.activation(out=pT[hb][:], in_=bank[hb][:], func=AF.Exp), reads=[], writes=[B(hb), ("pT", hb)])
            tr.op("dve", lambda e: e.tensor_tensor(out=pT[hb][:], in0=pT[hb][:], in1=Entab[:, hb, :], op=ALU.mult),
                  reads=["consts"], writes=[("pT", hb)])
        pts = sqb[0][:, 0:256]
        for q4 in range(4):
            b0 = 4 * q4
            for hh in range(2):
                tr.dma("pool", kcq[:, 2 * hh:2 * hh + 2, :, :],
                       kcd_d[L].rearrange("p (b n) -> p b n", b=NSEQ_S)[:, b0 + 2 * hh:b0 + 2 * hh + 2, :].rearrange("p b (k s) -> p b k s", k=4),
                       writes=["A_kc"], grp=("kc", L, q4))
                tr.dma("pool", vcq[:, 2 * hh:2 * hh + 2, :, :],
                       vcd_d[L].rearrange("p (b n) -> p b n", b=NSEQ_S)[:, b0 + 2 * hh:b0 + 2 * hh + 2, :].rearrange("p b (k s) -> p b k s", k=4),
                       writes=["A_vc"], grp=("vc", L, q4))
            for b4 in range(4):
                for kvh in range(KVH):
                    for par in range(2):
                        col = b4 * 64 + kvh * 16 + par * 8
                        rq = qzs[:, par, 2 * kvh:2 * kvh + 2, :].rearrange("p c (i b) -> p c i b", b=NSEQ_S)[:, :, :, b0 + b4]
                        tr.op("pe", lambda e: e.matmul(bank[2][:, col:col + 8], lhsT=kcq[:, b4, kvh, :], rhs=rq, start=True, stop=True),
                              reads=["A_kc", "qzs"], writes=[B(2)], inc=(b4 == 3 and kvh == 3 and par == 1))
            tr.op("act", lambda e: e.activation(out=pts, in_=bank[2][:, 0:256], func=AF.Exp), reads=[], writes=[B(2), ("sqb", 0)])
            ep = Esamp[:, :, :].unsqueeze(1).broadcast_to([128, 4, 16, 4])
            tr.op("dve", lambda e: e.tensor_tensor(out=pts.rearrange("p (b h i) -> p b h i", b=4, h=16), in0=pts.rearrange("p (b h i) -> p b h i", b=4, h=16),
                                                   in1=ep, op=ALU.mult), reads=["consts"], writes=[("sqb", 0)])
            for b4 in range(4):
                for kvh in range(KVH):
                    col = b4 * 64 + kvh * 16
                    pn = pT[kvh // 2][:, (kvh % 2) * 256:(kvh % 2) * 256 + 256].rearrange("p (h i b) -> p h i b", h=4, i=4)[:, :, :, b0 + b4]
                    lastg = (b4 == 3 and kvh == 3)
                    tr.op("pe", lambda e: e.matmul(bank[4][:, col:col + 16], lhsT=vcq[:, b4, kvh, :], rhs=pts[:, col:col + 16],
                                                   start=True, stop=False), reads=["A_vc", ("sqb", 0)], writes=[B(4)], inc=False)
                    tr.op("pe", lambda e: e.matmul(bank[4][:, col:col + 16], lhsT=vsdup[:, kvh, :], rhs=pn, start=False, stop=True),
                          reads=["vsdup", ("pT", 0), ("pT", 1)], writes=[B(4)], inc=False)
                    tr.op("pe", lambda e: e.matmul(bank[5][:, col:col + 16], lhsT=ones_b[:], rhs=pts[:, col:col + 16],
                                                   start=True, stop=False), reads=["ones"], writes=[B(5)], inc=False)
                    tr.op("pe", lambda e: e.matmul(bank[5][:, col:col + 16], lhsT=ones_b[:], rhs=pn, start=False, stop=True),
                          reads=[], writes=[B(5)], inc=lastg)
            rs_ = rec[0][:, 0:256] if q4 % 2 == 0 else rec[1][:, 0:256]
            rk = ("rec", q4 % 2)
            l5 = lambda a: a.rearrange("p (b k r c i) -> p b k r c i", b=4, k=4, r=2, c=2)
            for kvh in range(KVH):
                for c2 in range(2):
                    tr.op("dve", lambda e: e.tensor_scalar(out=l5(rs_)[:, :, kvh, :, c2, :], in0=l5(bank[5][:, 0:256])[:, :, kvh, :, c2, :],
                                                           scalar1=sinkexp[:, L, kvh, c2:c2 + 1], scalar2=None, op0=ALU.add),
                          reads=["consts"], writes=[B(5), rk])
            tr.op("dve", lambda e: e.reciprocal(out=rs_, in_=rs_), reads=[], writes=[rk])
            for par in range(2):
                rows = slice(64 * par, 64 * par + 64)
                for kvh in range(KVH):
                    for c2 in range(2):
                        zo = z[rows, 8 + 2 * kvh + c2, SC].rearrange("p (i b) -> p b i", b=NSEQ_S)[:, b0:b0 + 4, :]
                        tr.op("dve", lambda e: e.tensor_tensor(out=zo, in0=l5(bank[4][rows, 0:256])[:, :, kvh, par, c2, :],
                                                               in1=l5(rs_[rows, :])[:, :, kvh, par, c2, :], op=ALU.mult),
                              reads=[rk], writes=[B(4), ("z", 8 + 2 * kvh + c2)])
        tr.handoff(AK, ZU)

    def ssm(L, t, S):
        W_ = TM + S
        xnf = xn[:, :, :].rearrange("p c t -> p (c t)")
        mats = xnf[:, 0:4096].rearrange("p (k g m) -> p k g m", k=4, g=8)
        matsf = xnf[:, 0:4096]
        wtab = xnf[:, 4096:6144].bitcast(F32).rearrange("p (w n) -> p w n", w=2)
        Vr = xnf[:, 6144:7168].bitcast(F32)
        Vrs = xnf[:, 7168:8192].bitcast(F32)
        XN = [("xn", c) for c in range(KC)]
        SK = ["S_matsA", ("S_matsB", 0), ("S_wtab", 0), "S_Vr", "S_Vrs"]
        tr.handoff(XN, SK)
        ZD = [("z", c) for c in range(16, 22)]
        SK2 = [("S_matsB", 1), ("S_wtab", 1), "S_A8"]
        tr.handoff(ZD, SK2)
        A8 = z[:, 16, 0:512]
        matsB2 = z[:, 17:19, :].rearrange("p c t -> p (c t)")[:, 0:1024].bitcast(BF16).rearrange("p (k g m) -> p k g m", k=2, g=8)
        wtab2 = z[:, 19:21, :].rearrange("p c t -> p (c t)")[:, 0:1024].rearrange("p (w n) -> p w n", w=2)
        matsBv = [mats[:, 2:4, :, :], matsB2]
        wtabv = [wtab, wtab2]
        tA_ = tmp[0][:, 0:512]
        Yc = tmp[1][:, 0:512]
        Hb = sqb[1][:, 0:512].rearrange("p (g j) -> p g j", g=8)
        h0c = pT[0][:, :].bitcast(F32).rearrange("p (a g b) -> p a g b", a=2, g=8)
        hsn = pT[1][:, 0:256].bitcast(F32).rearrange("p (g b) -> p g b", g=8)
        Ycs = pT[1][:, 256:512].bitcast(F32)
        Hfull = rstd[:, 0:520].rearrange("p (g j) -> p g j", g=8)
        def restack_x(c):
            tr.dma("sp", UB_D[c], ub[:, c, 0:TM], reads=[("ub", c)], writes=[("UBD", c)])
            ubv = UB_D[c].rearrange("(g k) (s j) -> s k g j", k=16, s=8)
            for s_ in range(8):
                tr.dma("sp", Xc[c % 2][16 * s_:16 * s_ + 16, :, :], ubv[s_], reads=[("UBD", c)], writes=[("Xc", c % 2)], grp=("X", L, t, c))
        restack_x(0)
        for c in range(8):
            X = Xc[c % 2]
            if c + 1 < 8:
                restack_x(c + 1)
            pb_ = c % 2
            mB, wt_ = matsBv[pb_], wtabv[pb_]
            KB_, KW_ = ("S_matsB", pb_), ("S_wtab", pb_)
            tr.dma("act", mB, MATS_D[L, c][:, 2048:4096].rearrange("p (k g m) -> p k g m", k=2, g=8), writes=[KB_])
            tr.dma("act", wt_, WTAB_D[L, c].rearrange("p (w n) -> p w n", w=2), writes=[KW_])
            tr.dma("act", matsf[:, 0:2048], MATS_D[L, c][:, 0:2048], writes=["S_matsA"])
            for kk, bk in ((0, 0), (1, 1)):
                for g8 in range(8):
                    tr.op("pe", lambda e: e.matmul(bank[bk][:, g8 * 64:(g8 + 1) * 64], lhsT=mB[:, kk, g8, :], rhs=X[:, g8, :],
                                                   start=True, stop=True),
                          reads=[KB_, ("Xc", c % 2)], writes=[B(bk)], inc=(g8 == 7))
            Wa, Wb = wt_[:, 0, :], wt_[:, 1, :]
            DV = lambda fn, rd, wr: tr.op("dve", fn, reads=rd, writes=wr)
            tA2 = tmp[1][:, 0:512]
            gsl = slice(8 * c, 8 * c + 8)
            v3 = lambda a: a.rearrange("p (g j) -> p g j", g=8)
            l3 = lambda a: v3(a)[:, :, 63]
            DV(lambda e: e.tensor_tensor(out=Vr, in0=bank[0][:], in1=Wa, op=ALU.mult), [KW_], [B(0), "S_Vr"])
            DV(lambda e: e.tensor_tensor(out=tA_, in0=bank[1][:], in1=Wb, op=ALU.mult), [KW_], [B(1), ("tmp", 0)])
            DV(lambda e: e.tensor_tensor(out=Vrs, in0=bank[1][:], in1=Wa, op=ALU.mult), [KW_], [B(1), "S_Vrs"])
            DV(lambda e: e.tensor_tensor(out=tA2, in0=bank[0][:], in1=Wb, op=ALU.mult), [KW_], [B(0), ("tmp", 1)])
            DV(lambda e: e.tensor_copy(out=v3(A8), in_=r8[:, L, gsl].unsqueeze(2).broadcast_to([128, 8, 64])), ["consts"], ["S_A8"])
            DV(lambda e: e.tensor_tensor(out=Vr, in0=Vr, in1=tA_, op=ALU.add), [("tmp", 0)], ["S_Vr"])
            DV(lambda e: e.tensor_tensor(out=Vrs, in0=Vrs, in1=tA2, op=ALU.subtract), [("tmp", 1)], ["S_Vrs"])
            DV(lambda e: e.memset(v3(A8)[:, :, 0], 0.0), [], ["S_A8"])
            DV(lambda e: e.tensor_tensor(out=rec[0][:, 0:8], in0=hin[L][:, gsl], in1=r8[:, L, gsl], op=ALU.mult), [("hin", L), "consts"], [("rec", 0)])
            DV(lambda e: e.tensor_tensor(out=rec[1][:, 0:8], in0=hinsw[L][:, gsl], in1=r8[:, L, gsl], op=ALU.mult), [("hin", L), "consts"], [("rec", 1)])
            DV(lambda e: e.tensor_copy(out=Hfull[:, :, 0], in_=hin[L][:, gsl]), [("hin", L)], ["rstd"])
            DV(lambda e: e.tensor_tensor(out=v3(Vr)[:, :, 0], in0=v3(Vr)[:, :, 0], in1=rec[0][:, 0:8], op=ALU.add), [("rec", 0)], ["S_Vr"])
            DV(lambda e: e.tensor_tensor(out=v3(Vrs)[:, :, 0], in0=v3(Vrs)[:, :, 0], in1=rec[1][:, 0:8], op=ALU.add), [("rec", 1)], ["S_Vrs"])
            DV(lambda e: e.tensor_tensor_scan(out=Vr, data0=A8, data1=Vr, initial=0.0, op0=ALU.mult, op1=ALU.add), ["S_A8"], ["S_Vr"])
            DV(lambda e: e.tensor_tensor_scan(out=Vrs, data0=A8, data1=Vrs, initial=0.0, op0=ALU.mult, op1=ALU.add), ["S_A8"], ["S_Vrs"])
            DV(lambda e: e.tensor_tensor(out=Hfull[:, :, 1:65], in0=v3(Vr), in1=v3(Wa), op=ALU.mult), ["S_Vr", KW_], ["rstd"])
            DV(lambda e: e.tensor_tensor(out=tA_, in0=Vrs, in1=Wb, op=ALU.mult), ["S_Vrs", KW_], [("tmp", 0)])
            DV(lambda e: e.tensor_tensor(out=hinsw[L][:, gsl], in0=l3(Vrs), in1=l3(Wa), op=ALU.mult), ["S_Vrs", KW_], [("hin", L)])
            DV(lambda e: e.tensor_tensor(out=rec[0][:, 0:8], in0=l3(Vr), in1=l3(Wb), op=ALU.mult), ["S_Vr", KW_], [("rec", 0)])
            DV(lambda e: e.tensor_tensor(out=Hfull[:, :, 1:65], in0=Hfull[:, :, 1:65], in1=v3(tA_), op=ALU.subtract), [("tmp", 0)], ["rstd"])
            DV(lambda e: e.tensor_tensor(out=hinsw[L][:, gsl], in0=hinsw[L][:, gsl], in1=rec[0][:, 0:8], op=ALU.add), [("rec", 0)], [("hin", L)])
            DV(lambda e: e.tensor_copy(out=hin[L][:, gsl], in_=Hfull[:, :, 64]), ["rstd"], [("hin", L)])
            DV(lambda e: e.tensor_copy(out=Hb[:], in_=Hfull[:, :, 0:64]), ["rstd"], [("sqb", 1)])
            for g8 in range(8):
                tr.op("pe", lambda e: e.matmul(bank[2][:, g8 * 64:(g8 + 1) * 64], lhsT=mats[:, 0, g8, :], rhs=X[:, g8, :],
                                               start=True, stop=False), reads=["S_matsA", ("Xc", c % 2)], writes=[B(2)], inc=False)
                tr.op("pe", lambda e: e.matmul(bank[2][:, g8 * 64:(g8 + 1) * 64], lhsT=mats[:, 1, g8, :], rhs=Hb[:, g8, :],
                                               start=False, stop=True), reads=[("sqb", 1)], writes=[B(2)], inc=(g8 == 7))
            tr.op("act", lambda e: e.activation(out=Yc[:], in_=bank[2][:], func=AF.Copy), reads=[], writes=[B(2), ("tmp", 1)])
            ydv = YD[c].rearrange("(g k) (t j) -> t k g j", k=16, t=8)
            for t_ in range(8):
                tr.dma("sp", ydv[t_], Yc[16 * t_:16 * t_ + 16, :].rearrange("p (g j) -> p g j", g=8), reads=[("tmp", 1)], writes=[("YD", c)], grp=("Y", L, t, c))
            tr.dma("sp", z[:, c, 0:TM], YD[c], reads=[("YD", c)], writes=[("z", c)])
            if S:
                tr.dma("sp", UBS_D[c], ub[:, c, TM:TM + S], reads=[("ub", c)], writes=[("UBSD", c)])
                usv = UBS_D[c].rearrange("(g k) (i b) -> i k g b", k=16, i=4)
                for i_ in range(4):
                    tr.dma("sp", Xs[64 + 16 * i_:64 + 16 * i_ + 16, :, :], usv[i_], reads=[("UBSD", c)], writes=["Xs"], grp=("Xs", L, c))
                hv = lambda d: d[L].rearrange("p (g b) -> p g b", b=NSEQ_S)[:, gsl, :]
                tr.dma("sp", h0c[:, 0, :, :], hv(h0_d), writes=[("pT", 0)], grp=("h0", L, c))
                tr.dma("sp", h0c[:, 1, :, :], hv(h0sw_d), writes=[("pT", 0)], grp=("h0", L, c))
                DV(lambda e: e.tensor_copy(out=h0b[:], in_=h0c[:, 0, :, :]), [("pT", 0)], ["h0b"])
                for g8 in range(8):
                    tr.op("pe", lambda e: e.matmul(bank[3][:, g8 * 16:(g8 + 1) * 16], lhsT=mB[:, 0, g8, :], rhs=Xs[:, g8, :],
                                                   start=True, stop=True), reads=["S_matsA", KB_, "Xs"], writes=[B(3)], inc=False)
                for g8 in range(8):
                    o_ = 128 + g8 * 16
                    rsh = matsf[:, (8 + g8) * 128 - 64:(8 + g8) * 128 + 64]
                    tr.op("pe", lambda e: e.matmul(bank[3][:, o_:o_ + 16], lhsT=mats[:, 0, g8, :], rhs=Xs[:, g8, :],
                                                   start=True, stop=False), reads=[], writes=[B(3)], inc=False)
                    tr.op("pe", lambda e: e.matmul(bank[3][:, o_:o_ + 16], lhsT=rsh, rhs=h0b[:, g8, :],
                                                   start=False, stop=True), reads=["h0b"], writes=[B(3)], inc=(g8 == 7))
                pw = lambda i: pw4[:, L, i, gsl].unsqueeze(2).broadcast_to([128, 8, 16])
                DV(lambda e: e.tensor_tensor(out=hsn[:], in0=h0c[:, 0, :, :], in1=pw(0), op=ALU.mult), [("pT", 0), "consts"], [("pT", 1)])
                DV(lambda e: e.tensor_tensor(out=h0c[:, 0, :, :], in0=h0c[:, 1, :, :], in1=pw(1), op=ALU.mult), ["h0b", "consts"], [("pT", 0)])
                DV(lambda e: e.tensor_tensor(out=hsn[:], in0=hsn[:], in1=h0c[:, 0, :, :], op=ALU.add), [], [("pT", 1), ("pT", 0)])
                DV(lambda e: e.tensor_tensor(out=hsn[:], in0=hsn[:], in1=bank[3][:, 0:128].rearrange("p (g b) -> p g b", g=8), op=ALU.add),
                   [], [B(3), ("pT", 1)])
                tr.dma("sp", HS_D[L].rearrange("p (g b) -> p g b", b=NSEQ_S)[:, gsl, :], hsn[:], reads=[("pT", 1)])
                tr.op("act", lambda e: e.activation(out=Ycs[64:128, :], in_=bank[3][64:128, 128:256], func=AF.Copy),
                      reads=[], writes=[B(3), ("pT", 1)])
                ysv = YDS[c].rearrange("(g k) (i b) -> i k g b", k=16, i=4)
                for i_ in range(4):
                    tr.dma("sp", ysv[i_], Ycs[64 + 16 * i_:64 + 16 * i_ + 16, :].rearrange("p (g b) -> p g b", g=8),
                           reads=[("pT", 1)], writes=[("YDS", c)], grp=("Ys", L, c))
                tr.dma("sp", z[:, c, TM:TM + S], YDS[c], reads=[("YDS", c)], writes=[("z", c)])
        tr.handoff(SK, XN)
        tr.handoff(SK2, ZD)
        for c in range(8):
            y = z[:, c, 0:W_]
            t1, sg, ygb = tmp[0][:, 0:W_], tmp[1][:, 0:W_], sqb[0][:, 0:W_]
            DV = lambda fn, rd, wr: tr.op("dve", fn, reads=rd, writes=wr)
            DV(lambda e: e.scalar_tensor_tensor(out=y, in0=ub[:, c, 0:W_], scalar=Dv[:, L, c:c + 1], in1=y, op0=ALU.mult, op1=ALU.add),
               [("ub", c), "consts"], [("z", c)])
            DV(lambda e: e.tensor_tensor(out=t1, in0=y, in1=y, op=ALU.mult), [("z", c)], [("tmp", 0)])
            DV(lambda e: e.tensor_scalar(out=t1, in0=t1, scalar1=0.044715, scalar2=1.0, op0=ALU.mult, op1=ALU.add), [], [("tmp", 0)])
            DV(lambda e: e.tensor_tensor(out=t1, in0=t1, in1=y, op=ALU.mult), [("z", c)], [("tmp", 0)])
            tr.op("act", lambda e: e.activation(out=t1, in_=t1, func=AF.Sigmoid, scale=1.5957691216057308), reads=[], writes=[("tmp", 0)])
            DV(lambda e: e.tensor_tensor(out=y, in0=y, in1=t1, op=ALU.mult), [("tmp", 0)], [("z", c)])
            DV(lambda e: e.tensor_copy(out=ygb, in_=y), [("z", c)], [("sqb", 0)])
            tr.op("pe", lambda e: e.matmul(bank[6][:, 0:TM], lhsT=wglu[:, L, c, :], rhs=ygb[:, 0:TM], start=True, stop=True),
                  reads=[("sqb", 0), "consts"], writes=[B(6)])
            tr.op("act", lambda e: e.activation(out=sg[:, 0:TM], in_=bank[6][:, 0:TM], func=AF.Sigmoid), reads=[], writes=[B(6), ("tmp", 1)])
            if S:
                tr.op("pe", lambda e: e.matmul(bank[7][:, 0:S], lhsT=wglu[:, L, c, :], rhs=ygb[:, TM:W_], start=True, stop=True),
                      reads=[("sqb", 0), "consts"], writes=[B(7)])
                tr.op("act", lambda e: e.activation(out=sg[:, TM:W_], in_=bank[7][:, 0:S], func=AF.Sigmoid), reads=[], writes=[B(7), ("tmp", 1)])
            DV(lambda e: e.tensor_tensor(out=y, in0=y, in1=sg, op=ALU.mult), [("tmp", 1)], [("z", c)])

    def mixer(L, t, S):
        W_ = TM + S
        rmsnorm(x, KC, "mix_norm", L, xn, S, D, ["x"])
        xn_keys = [("xn", c) for c in range(KC)]

        def evac_z(m, pb):
            if m < 8:
                tr.op("act", lambda e: e.activation(out=ub[:, m, 0:TM].rearrange("p (s j) -> p j s", s=8),
                                                    in_=bank[pb][:].rearrange("p (j s) -> p j s", s=8), func=AF.Copy),
                      reads=[], writes=[B(pb), ("ub", m)])
                if S:
                    tr.op("act", lambda e: e.activation(out=ub[:, m, TM:TM + S], in_=bank[4 + pb][:, 0:S], func=AF.Copy),
                          reads=[], writes=[B(4 + pb), ("ub", m)])
                return
            tr.op("act", lambda e: e.activation(out=z[:, m, 0:TM], in_=bank[pb][:], func=AF.Copy), reads=[], writes=[B(pb), ("z", m)])
            if S:
                tr.op("act", lambda e: e.activation(out=z[:, m, TM:TM + S], in_=bank[4 + pb][:, 0:S], func=AF.Copy),
                      reads=[], writes=[B(4 + pb), ("z", m)])
        linear(xn, KC, xn_keys, INW, S, evac_z)
        attention(L, t, S)
        ssm(L, t, S)
        zk = [("z", m) for m in range(ZC)]
        rmsnorm(z, 8, "ssm_out_norm", L, xn, S, 1024, zk, coff=0, doff=0, perm=True)
        rmsnorm(z, 8, "attn_out_norm", L, xn, S, 1024, zk, coff=8, doff=8)
        mk = [("xn", c) for c in range(KC)]

        def evac_o(d, pb):
            tr.op("dve", lambda e: e.tensor_tensor(out=x[:, d, 0:TM], in0=bank[pb][:], in1=x[:, d, 0:TM], op=ALU.add),
                  reads=[], writes=[B(pb), "x"])
            if S:
                tr.op("dve", lambda e: e.tensor_tensor(out=x[:, d, TM:TM + S], in0=bank[4 + pb][:, 0:S], in1=x[:, d, TM:TM + S], op=ALU.add),
                      reads=[], writes=[B(4 + pb), "x"])
        linear(xn, KC, mk, D, S, evac_o)

    xv = xTp.rearrange("(c p) t -> p c t", p=128)
    yv = yTp.rearrange("(c p) t -> p c t", p=128)
    for t in range(NT):
        S = TS if t == NT - 1 else 0
        for c0 in range(0, KC, 4):
            tr.dma("sp" if (c0 // 4) % 2 == 0 else "act", x[:, c0:c0 + 4, 0:TM], xv[:, c0:c0 + 4, t * TM:(t + 1) * TM], writes=["x"], grp=("x", t))
        if S:
            tr.dma("sp", x[:, :, TM:TM + S], xTs.rearrange("(c p) t -> p c t", p=128), writes=["x"], grp=("x", t))
        for L in range(DEPTH):
            ffn(L, 1, S)
            mixer(L, t, S)
            ffn(L, 2, S)
        for c0 in range(0, KC, 4):
            tr.dma("sp" if (c0 // 4) % 2 == 0 else "act", yv[:, c0:c0 + 4, t * TM:(t + 1) * TM], x[:, c0:c0 + 4, 0:TM], reads=["x"])
        if S:
            tr.dma("sp", yTs.rearrange("(c p) t -> p c t", p=128), x[:, :, TM:TM + S], reads=["x"])
    for L in range(DEPTH):
        tr.dma("sp", HP_D[L], hin[L][:], reads=[("hin", L)])
    tr.finish()
    return nc


def host_consts():
    h = np.arange(NH, dtype=np.float64)
    slopes = 2.0 ** (-8.0 * (h + 1) / NH)
    sidx = np.arange(128)[:, None]
    qidx = np.arange(128)[None, :]
    E = np.zeros((128, 2, 4, 2, 2, 128), np.float64)
    Es = np.zeros((128, 4, 2, 2, 4), np.float64)
    for kvh in range(4):
        for par in range(2):
            for c2 in range(2):
                m = slopes[4 * kvh + 2 * c2 + par]
                E[:, 1, kvh, par, c2, :] = np.where(sidx <= qidx, -m * (qidx - sidx), -30000.0)
                E[:, 0, kvh, par, c2, :] = np.where(sidx > qidx, -m * (qidx + 128 - sidx), -30000.0)
                Es[:, kvh, par, c2, :] = np.where(sidx > qidx[:, :4], np.exp(-m * (qidx[:, :4] + 128 - sidx)), 0.0)
    bd = np.zeros((128, 128), np.float32)
    bd[:64, :64] = 1.0
    bd[64:, 64:] = 1.0
    En = np.zeros((128, 4, 2, 2, 4, 16), np.float64)
    for kvh in range(4):
        for par in range(2):
            for c2 in range(2):
                m = slopes[4 * kvh + 2 * c2 + par]
                for ip in range(4):
                    for i in range(ip, 4):
                        for b in range(16):
                            En[64 + 16 * ip + b, kvh, par, c2, i, b] = np.exp(-m * (i - ip))
    sgn = np.ones((128, 2), np.float32); sgn[:64] = -1.0
    tt = np.arange(128) // 16
    mask_ts = (tt[None, :] >= tt[:, None]).astype(np.float32)
    kvals = np.concatenate([np.arange(7, -8, -1), np.arange(0, 9)]).astype(np.float32)
    jvals = (1.0 * (np.arange(64) + 1)).astype(np.float32)
    return {"Etab": E.reshape(128, -1).astype(np.float32), "ident": np.eye(128, dtype=np.float32), "bdones": bd, "Entab": En.reshape(128, -1).astype(np.float32), "Esamp": Es.reshape(128, -1).astype(np.float32),
            "sgn": sgn, "mask_ts": mask_ts, "kvals": np.tile(kvals, (128, 1)), "jvals": np.tile(jvals, (128, 1))}


def host_layouts(inp):
    f32 = lambda a: np.ascontiguousarray(np.asarray(a, dtype=np.float32))
    w_in = np.asarray(inp["w_in"], dtype=np.float32)
    parts = [w_in[:, :, 0:2048]]
    for kvh in range(4):
        kk = w_in[:, :, 2048 + kvh * 64:2048 + (kvh + 1) * 64]
        parts += [kk, kk]
    parts.append(w_in[:, :, 2304:2560])
    out = {"w_in_aug": f32(np.concatenate(parts, axis=-1))}
    gq = np.asarray(inp["q_norm"], dtype=np.float32)
    gk = np.asarray(inp["k_norm"], dtype=np.float32)
    gqk = np.zeros((128, DEPTH, 2), np.float32)
    for L in range(DEPTH):
        gqk[:, L, 0] = np.tile(gq[L], 2)
        gqk[:, L, 1] = np.tile(gk[L], 2)
    out["gqk"] = gqk.reshape(128, -1)
    sk = np.asarray(inp["sinks"], dtype=np.float32)
    sr = np.zeros((128, DEPTH, 4, 2), np.float32)
    for L in range(DEPTH):
        for kvh in range(4):
            for c2 in range(2):
                sr[:64, L, kvh, c2] = sk[L, 4 * kvh + 2 * c2]
                sr[64:, L, kvh, c2] = sk[L, 4 * kvh + 2 * c2 + 1]
    out["sinks_rep"] = sr.reshape(128, -1)
    for k in ["ffn1_w_gate", "ffn1_w_up", "ffn1_w_down", "ffn2_w_gate", "ffn2_w_up", "ffn2_w_down",
              "w_out", "ffn1_norm", "mix_norm", "ffn2_norm", "ssm_out_norm", "attn_out_norm"]:
        out[k] = f32(inp[k])
    tp = lambda a, ax: np.asarray(a, dtype=np.float32).transpose(ax)
    Are = tp(inp["ssm_A_re"], (0, 2, 1)); Aim = tp(inp["ssm_A_im"], (0, 2, 1))
    out["Are"] = f32(np.concatenate([Are, Are], 1)); out["Aim"] = f32(np.concatenate([Aim, Aim], 1))
    out["ldt"] = f32(np.broadcast_to(np.asarray(inp["ssm_log_dt"], dtype=np.float32)[:, None, :], (DEPTH, 128, 64)))
    Br = tp(inp["ssm_B_re"], (0, 2, 1, 3)).reshape(DEPTH, 64, 1024); Bi = tp(inp["ssm_B_im"], (0, 2, 1, 3)).reshape(DEPTH, 64, 1024)
    Cr = tp(inp["ssm_C_re"], (0, 3, 1, 2)).reshape(DEPTH, 64, 1024); Ci = tp(inp["ssm_C_im"], (0, 3, 1, 2)).reshape(DEPTH, 64, 1024)
    out["Bq1"] = f32(np.concatenate([Br, Bi], 1)); out["Bq2"] = f32(np.concatenate([Bi, Br], 1))
    out["Cq1"] = f32(np.concatenate([Cr, Ci], 1)); out["Cq2"] = f32(np.concatenate([Ci, Cr], 1))
    Dr = np.asarray(inp["ssm_D"], dtype=np.float32).reshape(DEPTH, 8, 128)
    out["Dv"] = f32(Dr.transpose(2, 0, 1).reshape(128, -1))
    wg = np.asarray(inp["ssm_w_glu"], dtype=np.float32)
    bdw = np.zeros((128, DEPTH, 8, 128), np.float32)
    for L in range(DEPTH):
        for g in range(64):
            c_, g8 = g // 8, g % 8
            bdw[16 * g8:16 * g8 + 16, L, c_, 16 * g8:16 * g8 + 16] = wg[L, g]
    out["wglu_bd"] = bdw.reshape(128, -1)
    out.update(host_consts())
    return out


def host_state(inp, c):
    re = np.asarray(inp["state_ssm_re"], dtype=np.float32)[:, c * NSEQ_S:(c + 1) * NSEQ_S]
    im = np.asarray(inp["state_ssm_im"], dtype=np.float32)[:, c * NSEQ_S:(c + 1) * NSEQ_S]
    rt, it = re.transpose(0, 3, 2, 1), im.transpose(0, 3, 2, 1)
    h0 = np.concatenate([rt, it], 1).reshape(DEPTH, 128, -1)
    h0sw = np.concatenate([it, rt], 1).reshape(DEPTH, 128, -1)
    return np.ascontiguousarray(h0), np.ascontiguousarray(h0sw)


def host_cache(inp, c):
    ck = np.asarray(inp["cache_k"], dtype=np.float32)[:, c * NSEQ_S:(c + 1) * NSEQ_S]
    cv = np.asarray(inp["cache_v"], dtype=np.float32)[:, c * NSEQ_S:(c + 1) * NSEQ_S]
    kt = ck.transpose(0, 4, 1, 3, 2)
    kcd = np.concatenate([kt, kt], 1).reshape(DEPTH, 128, -1)
    vt = cv.transpose(0, 2, 1, 3, 4)
    vcd = np.concatenate([vt, vt], -1).reshape(DEPTH, 128, -1)
    return {"kcd": np.ascontiguousarray(kcd), "vcd": np.ascontiguousarray(vcd),
            "ck_raw": np.ascontiguousarray(ck.reshape(DEPTH, NSEQ_S, 128, 256)),
            "cv_raw": np.ascontiguousarray(cv.reshape(DEPTH, NSEQ_S, 128, 256))}


_NC = None


def kernel(**inp):
    global _NC
    f32 = lambda a: np.ascontiguousarray(np.asarray(a, dtype=np.float32))
    if _NC is None:
        _NC = build()
    xp = np.asarray(inp["x_prompt"], dtype=np.float32)
    xs = np.asarray(inp["x_sample"], dtype=np.float32)
    shared = host_layouts(inp)
    in_maps = []
    for c in range(NCORES):
        m = dict(shared)
        m["xTp"] = f32(xp[c % NB].T)
        m["xTs"] = f32(xs[c * NSEQ_S:(c + 1) * NSEQ_S].transpose(1, 0, 2).reshape(NSEQ_S * LS, D).T)
        m["h0"], m["h0sw"] = host_state(inp, c)
        m.update(host_cache(inp, c))
        in_maps.append(m)
    res = run_bass_kernel_spmd(_NC, in_maps, core_ids=list(range(NCORES)))
    R = res.results
    y_prompt = np.stack([R[b]["yTp"].T for b in range(NB)]).astype(np.float32)
    y_sample = np.concatenate([R[c]["yTs"].T.reshape(LS, NSEQ_S, D).transpose(1, 0, 2) for c in range(NCORES)]).astype(np.float32)
    A = lambda a: np.asarray(a, dtype=np.float32)
    prompt_k = np.stack([A(R[b]["PK_D"]).reshape(DEPTH, 128, 4, 64) for b in range(NB)], axis=1)
    prompt_v = np.stack([A(R[b]["PV_D"]).reshape(DEPTH, 128, 4, 64) for b in range(NB)], axis=1)
    hp = np.stack([A(R[b]["HP_D"]) for b in range(NB)], axis=1)
    prompt_re = np.ascontiguousarray(hp[:, :, :64, :].transpose(0, 1, 3, 2))
    prompt_im = np.ascontiguousarray(hp[:, :, 64:, :].transpose(0, 1, 3, 2))
    sample_k = np.concatenate([A(R[c]["SK_D"]).reshape(DEPTH, NSEQ_S, 128, 4, 64) for c in range(NCORES)], axis=1)
    sample_v = np.concatenate([A(R[c]["SV_D"]).reshape(DEPTH, NSEQ_S, 128, 4, 64) for c in range(NCORES)], axis=1)
    hs = np.concatenate([A(R[c]["HS_D"]).reshape(DEPTH, 128, 64, NSEQ_S) for c in range(NCORES)], axis=3)
    sample_re = np.ascontiguousarray(hs[:, :64].transpose(0, 3, 2, 1))
    sample_im = np.ascontiguousarray(hs[:, 64:].transpose(0, 3, 2, 1))
    return (y_prompt, y_sample, prompt_k, prompt_v, prompt_re, prompt_im, sample_k, sample_v, sample_re, sample_im)
```

```python
import os
import numpy as np
import concourse.bass as bass
import concourse.mybir as mybir
from concourse.bass_utils import run_bass_kernel_spmd

F32 = mybir.dt.float32
BF16 = mybir.dt.bfloat16
AF = mybir.ActivationFunctionType
ALU = mybir.AluOpType
AX = mybir.AxisListType

D, DFF, DEPTH = 2048, 5632, 2
SEQ, NB, NSEQ_S, LS = 2048, 4, 16, 4
TM, TS = 512, 64
NT = SEQ // TM
T = TM + TS
KC, FC = D // 128, DFF // 128
INW = 2816
ZC = INW // 128
KVH, HD, NH = 4, 64, 16
EPS = 1e-6
NCORES = 8


class Tracker:
    def __init__(self, nc, n_dma_sems=24):
        self.nc = nc
        self.eng = {"pe": nc.tensor, "act": nc.scalar, "dve": nc.vector, "pool": nc.gpsimd, "sp": nc.sync}
        self.sem, self.cnt = {}, {}
        for e in self.eng:
            self.sem[e] = nc.alloc_semaphore(f"s_{e}")
            self.cnt[e] = 0
        self.seen = {e: {} for e in self.eng}
        self.bufs = {}
        self.pending = {e: ([], []) for e in self.eng}
        self.dma_sems = [nc.alloc_semaphore(f"s_dma{i}") for i in range(2 * n_dma_sems)]
        self.dma_val = [0] * (2 * n_dma_sems)
        self.dma_rr = {"pool": 0, "other": 0}
        self.n_dma = n_dma_sems
        self.n_pool = 16
        self.samesync = {"pe": False, "act": True, "dve": True, "pool": True, "sp": True}

    def _wait(self, e, ev):
        if ev is None:
            return
        sem, val, owner = ev
        if owner == e and not self.samesync[e]:
            return
        key = sem.name
        if self.seen[e].get(key, 0) >= val:
            return
        self.eng[e].wait_ge(sem, val)
        self.seen[e][key] = val

    def _deps(self, e, reads, writes, grp=None):
        for k in reads:
            b = self.bufs.get(k)
            if b is not None:
                for w in b[0]:
                    self._wait(e, w)
        for k in writes:
            b = self.bufs.get(k)
            if b is not None:
                if grp is not None and b[2] == grp:
                    continue
                for w in b[0]:
                    self._wait(e, w)
                for r in b[1]:
                    self._wait(e, r)

    def _commit(self, ev, reads, writes, grp=None):
        for k in reads:
            b = self.bufs.setdefault(k, [[], [], None])
            b[1].append(ev)
            if len(b[1]) > 48:
                b[1] = b[1][-48:]
        for k in writes:
            b = self.bufs.get(k)
            if grp is not None and b is not None and b[2] == grp:
                b[0].append(ev)
            else:
                self.bufs[k] = [[ev], [], grp]

    def op(self, e, fn, reads=(), writes=(), inc=True):
        self._deps(e, reads, writes)
        ins = fn(self.eng[e])
        pr, pw = self.pending[e]
        if not inc:
            pr.extend(reads)
            pw.extend(writes)
            return ins
        self.cnt[e] += 1
        ins.then_inc(self.sem[e], 1)
        ev = (self.sem[e], self.cnt[e], e)
        self._commit(ev, list(reads) + pr, list(writes) + pw)
        self.pending[e] = ([], [])
        return ins

    def dma(self, q, out, in_, reads=(), writes=(), grp=None, **kw):
        qk = "pool" if q == "pool" else "other"
        i = self.dma_rr[qk] + (0 if qk == "pool" else self.n_dma)
        self.dma_rr[qk] = (self.dma_rr[qk] + 1) % (self.n_pool if qk == "pool" else self.n_dma)
        sem = self.dma_sems[i]
        if self.dma_val[i] > 0:
            self._wait(q, (sem, self.dma_val[i], "dma"))
        self._deps(q, reads, writes, grp)
        self.dma_val[i] += 16
        self.eng[q].dma_start(out=out, in_=in_, **kw).then_inc(sem, 16)
        ev = (sem, self.dma_val[i], "dma")
        self._commit(ev, list(reads), list(writes), grp)
        return ev

    def barrier(self):
        for e in self.eng:
            for f in self.eng:
                if f != e and self.cnt[f] > 0:
                    self._wait(e, (self.sem[f], self.cnt[f], f))
            for i, sem in enumerate(self.dma_sems):
                if self.dma_val[i] > 0:
                    self._wait(e, (sem, self.dma_val[i], "dma"))
        self.bufs = {}

    def handoff(self, src_keys, dst_keys):
        evs = []
        for k in src_keys:
            b = self.bufs.get(k)
            if b is not None:
                evs.extend(b[0])
                evs.extend(b[1])
        for k in dst_keys:
            self.bufs[k] = [[], list(evs), None]

    def finish(self):
        for i, sem in enumerate(self.dma_sems):
            if self.dma_val[i] > 0:
                self._wait("sp", (sem, self.dma_val[i], "dma"))
        for e in self.eng:
            if e != "sp" and self.cnt[e] > 0:
                self._wait("sp", (self.sem[e], self.cnt[e], e))


def build(NT=NT):
    nc = bass.Bass("TRN2", target_bir_lowering=False)
    dt = lambda n, s, k="ExternalInput": nc.dram_tensor(n, s, F32, kind=k).ap()
    xTp = dt("xTp", [D, SEQ])
    xTs = dt("xTs", [D, TS])
    yTp = dt("yTp", [D, SEQ], "ExternalOutput")
    yTs = dt("yTs", [D, TS], "ExternalOutput")
    W = {}
    for nm, shp in [("ffn1_w_gate", [DEPTH, D, DFF]), ("ffn1_w_up", [DEPTH, D, DFF]), ("ffn1_w_down", [DEPTH, DFF, D]),
                    ("ffn2_w_gate", [DEPTH, D, DFF]), ("ffn2_w_up", [DEPTH, D, DFF]), ("ffn2_w_down", [DEPTH, DFF, D]),
                    ("w_in_aug", [DEPTH, D, INW]), ("w_out", [DEPTH, D, D]),
                    ("ffn1_norm", [DEPTH, D]), ("mix_norm", [DEPTH, D]), ("ffn2_norm", [DEPTH, D]),
                    ("ssm_out_norm", [DEPTH, 1024]), ("attn_out_norm", [DEPTH, 1024])]:
        W[nm] = dt(nm, shp)
    Etab_d = dt("Etab", [128, 2 * 4 * 512])
    Entab_d = dt("Entab", [128, 1024])
    Esamp_d = dt("Esamp", [128, 64])
    kcd_d = dt("kcd", [DEPTH, 128, NSEQ_S * 512])
    vcd_d = dt("vcd", [DEPTH, 128, NSEQ_S * 512])
    ckraw_d = dt("ck_raw", [DEPTH, NSEQ_S, 128, 256])
    cvraw_d = dt("cv_raw", [DEPTH, NSEQ_S, 128, 256])
    PK_D = dt("PK_D", [DEPTH, 128, 256], "ExternalOutput")
    PV_D = dt("PV_D", [DEPTH, 128, 256], "ExternalOutput")
    SK_D = dt("SK_D", [DEPTH, NSEQ_S, 128, 256], "ExternalOutput")
    SV_D = dt("SV_D", [DEPTH, NSEQ_S, 128, 256], "ExternalOutput")
    ident_d = dt("ident", [128, 128])
    bdones_d = dt("bdones", [128, 128])
    gqk_d = dt("gqk", [128, DEPTH * 2])
    sinks_d = dt("sinks_rep", [128, DEPTH * 8])
    ssm_in = {}
    for nm, shp in [("Are", [DEPTH, 128, 64]), ("Aim", [DEPTH, 128, 64]), ("ldt", [DEPTH, 128, 64]),
                    ("Bq1", [DEPTH, 128, 1024]), ("Bq2", [DEPTH, 128, 1024]), ("Cq1", [DEPTH, 128, 1024]), ("Cq2", [DEPTH, 128, 1024]),
                    ("sgn", [128, 2]), ("mask_ts", [128, 128]), ("kvals", [128, 24]), ("jvals", [128, 64])]:
        ssm_in[nm] = dt(nm, shp)
    dbg = bool(int(os.environ.get("DEBUG_SSM", "0")))
    skind = "ExternalOutput" if dbg else "Internal"
    MATS_D = nc.dram_tensor("MATS_D", [DEPTH, 8, 128, 4096], BF16, kind=skind).ap()
    WTAB_D = nc.dram_tensor("WTAB_D", [DEPTH, 8, 128, 1024], F32, kind=skind).ap()
    Dv_d = dt("Dv", [128, DEPTH * 8])
    wglu_d = dt("wglu_bd", [128, DEPTH * 8 * 128])
    h0_d = dt("h0", [DEPTH, 128, 64 * NSEQ_S])
    h0sw_d = dt("h0sw", [DEPTH, 128, 64 * NSEQ_S])
    HS_D = dt("HS_D", [DEPTH, 128, 64 * NSEQ_S], "ExternalOutput")
    HP_D = dt("HP_D", [DEPTH, 128, 64], "ExternalOutput")
    UB_D = nc.dram_tensor("UB_D", [8, 128, 512], BF16, kind="Internal").ap()
    YD = nc.dram_tensor("YD", [8, 128, 512], F32, kind="Internal").ap()
    UBS_D = nc.dram_tensor("UBS_D", [8, 128, 64], BF16, kind="Internal").ap()
    YDS = nc.dram_tensor("YDS", [8, 128, 64], F32, kind="Internal").ap()
    tr = Tracker(nc)
    sb = nc.alloc_sbuf_tensor
    ub = sb("ub", [128, 8, T], BF16)
    Xc = [sb(f"Xc{i}", [128, 8, 64], BF16) for i in range(2)]
    Xs = sb("Xs", [128, 8, 16], BF16)
    hin = [sb(f"hin{L}", [128, 64], F32) for L in range(DEPTH)]
    hinsw = [sb(f"hinsw{L}", [128, 64], F32) for L in range(DEPTH)]
    h0b = sb("h0b", [128, 8, 16], BF16)
    Dv = sb("Dv_s", [128, DEPTH, 8], F32)
    wglu = sb("wglu_s", [128, DEPTH, 8, 128], BF16)
    r8 = sb("r8", [128, DEPTH, 64], F32)
    pw4 = sb("pw4", [128, DEPTH, 2, 64], F32)
    Etab = sb("Etab_s", [128, 2, 4, 512], BF16)
    ident = sb("ident_s", [128, 128], F32)
    bdones = sb("bdones_s", [128, 128], BF16)
    gqk = sb("gqk_s", [128, DEPTH, 2], F32)
    sinkexp = sb("sinkexp_s", [128, DEPTH, 4, 2], F32)
    vtok = sb("vtok", [128, 4, 4, 128], BF16)
    kprev = [sb(f"kprev{L}", [128, 8, 128], BF16) for L in range(DEPTH)]
    vprev = [sb(f"vprev{L}", [128, 4, 128], BF16) for L in range(DEPTH)]
    qzs = sb("qzs", [128, 2, 8, 64], BF16)
    Esamp = sb("Esamp_s", [128, 16, 4], BF16)
    identb = sb("identb", [128, 128], BF16)
    Entab = sb("Entab_s", [128, 2, 512], BF16)
    vsdup = sb("vsdup", [128, 4, 128], BF16)
    pT = [sb(f"pT{i}", [128, 512], BF16) for i in range(2)]
    rec = [sb(f"rec{i}", [128, 256], F32) for i in range(2)]
    x = sb("x", [128, KC, T], F32)
    xn = sb("xn", [128, KC, T], BF16)
    big = sb("big", [128, FC * T // 2], F32)
    h = big[:, :].bitcast(BF16).rearrange("p (f t) -> p f t", f=FC)
    scr = big[:, 0:KC * T].rearrange("p (c t) -> p c t", c=KC)
    z = big[:, 0:ZC * T].rearrange("p (c t) -> p c t", c=ZC)
    gains = {nm: sb("g_" + nm, [128, DEPTH, n // 128], F32) for nm, n in
             [("ffn1_norm", D), ("mix_norm", D), ("ffn2_norm", D), ("ssm_out_norm", 1024), ("attn_out_norm", 1024)]}
    ones_b = sb("ones_b", [128, 128], BF16)
    sqb = [sb(f"sqb{i}", [128, T], BF16) for i in range(2)]
    rstd = sb("rstd", [128, T], F32)
    tmp = [sb(f"tmp{i}", [128, T], F32) for i in range(2)]
    rsb = tmp
    NSLOT = 3
    SLOTW = 8192
    wslot = [sb(f"wslot{i}", [128, SLOTW], BF16) for i in range(NSLOT)]
    bank = [nc.alloc_psum_tensor(f"bank{i}", [128, 512], F32) for i in range(8)]
    B = lambda i: ("B", i)

    with nc.allow_non_contiguous_dma(reason="tiny gain vectors"):
        for nm, g in gains.items():
            for L in range(DEPTH):
                tr.dma("sp", g[:, L, :], W[nm][L].rearrange("(c p) -> p c", p=128), writes=[("gain", nm)])
    tr.op("dve", lambda e: e.memset(ones_b[:], 1.0), writes=["ones"])
    tr.op("dve", lambda e: e.memset(qzs[:], 0.0), writes=["qzs"])
    tr.op("dve", lambda e: e.memset(vsdup[:], 0.0), writes=["vsdup"])
    tr.dma("pool", Entab[:], Entab_d.rearrange("p (k n) -> p k n", k=2), writes=["consts"])
    tr.dma("pool", Etab[:, 0, :, :], Etab_d[:, 0:2048].rearrange("p (k n) -> p k n", k=4), writes=["consts"])
    tr.dma("pool", Etab[:, 1, :, :], Etab_d[:, 2048:4096].rearrange("p (k n) -> p k n", k=4), writes=["consts"])
    tr.dma("pool", bdones[:], bdones_d, writes=["consts"])
    tr.dma("pool", identb[:], ident_d, writes=["consts"])
    tr.dma("pool", Esamp[:], Esamp_d.rearrange("p (h i) -> p h i", i=4), writes=["consts"])
    tr.dma("sp", ident[:], ident_d, writes=["consts"])
    tr.dma("sp", gqk[:], gqk_d.rearrange("p (l i) -> p l i", l=DEPTH), writes=["consts"])
    tr.dma("sp", sinkexp[:], sinks_d.rearrange("p (l k c) -> p l k c", l=DEPTH, k=4), writes=["consts"])
    tr.op("act", lambda e: e.activation(out=sinkexp[:], in_=sinkexp[:], func=AF.Exp), reads=[], writes=["consts"])
    tr.dma("sp", Dv[:], Dv_d.rearrange("p (l c) -> p l c", l=DEPTH), writes=["consts"])
    for L in range(DEPTH):
        tr.dma("pool", wglu[:, L, :, :], wglu_d[:, L * 1024:(L + 1) * 1024].rearrange("p (c m) -> p c m", c=8), writes=["consts"])
        tr.op("dve", lambda e: e.memset(hin[L][:], 0.0), writes=[("hin", L)])
        tr.op("dve", lambda e: e.memset(hinsw[L][:], 0.0), writes=[("hin", L)])
    tr.op("dve", lambda e: e.memset(Xs[:], 0.0), writes=["Xs"])


    TWO_PI = 6.283185307179586
    MAGIC = 12582912.0

    def ssm_setup(L):
        pools = [wslot[0][:, :].bitcast(F32), wslot[1][:, :].bitcast(F32), wslot[2][:, :].bitcast(F32), xn[:, :, :].rearrange("p c t -> p (c t)").bitcast(F32)]
        smalls = [tmp[0], tmp[1], rstd]
        st_ = {"pi": 0, "off": 0, "si": 0, "soff": 0}

        def carve(n):
            while st_["off"] + n > pools[st_["pi"]].shape[1]:
                st_["pi"] += 1
                st_["off"] = 0
            v = pools[st_["pi"]][:, st_["off"]:st_["off"] + n]
            st_["off"] += n
            return v

        def small():
            if st_["soff"] + 64 > T:
                st_["si"] += 1
                st_["soff"] = 0
            v = smalls[st_["si"]][:, st_["soff"]:st_["soff"] + 64]
            st_["soff"] += 64
            return v
        K_ = "setup"
        D_ = lambda fn: tr.op("dve", fn, reads=[], writes=[K_])
        A_ = lambda fn: tr.op("act", fn, reads=[], writes=[K_])
        bigf = big
        xf = x[:, :, :].rearrange("p c t -> p (c t)")

        def sincos(ang, cos_o, sin_o, tA, cycles=False):
            D_(lambda e: e.tensor_scalar(out=tA, in0=ang, scalar1=(1.0 if cycles else 1.0 / TWO_PI), scalar2=None, op0=ALU.mult))
            D_(lambda e: e.tensor_scalar(out=sin_o, in0=tA, scalar1=MAGIC, scalar2=None, op0=ALU.add))
            D_(lambda e: e.tensor_scalar(out=sin_o, in0=sin_o, scalar1=MAGIC, scalar2=None, op0=ALU.subtract))
            D_(lambda e: e.tensor_tensor(out=sin_o, in0=tA, in1=sin_o, op=ALU.subtract))
            D_(lambda e: e.tensor_scalar(out=cos_o, in0=tA, scalar1=0.25, scalar2=None, op0=ALU.add))
            D_(lambda e: e.tensor_scalar(out=tA, in0=cos_o, scalar1=MAGIC, scalar2=None, op0=ALU.add))
            D_(lambda e: e.tensor_scalar(out=tA, in0=tA, scalar1=MAGIC, scalar2=None, op0=ALU.subtract))
            D_(lambda e: e.tensor_tensor(out=cos_o, in0=cos_o, in1=tA, op=ALU.subtract))
            A_(lambda e: e.activation(out=sin_o, in_=sin_o, func=AF.Sin, scale=6.28318))
            A_(lambda e: e.activation(out=cos_o, in_=cos_o, func=AF.Sin, scale=6.28318))

        tAre, tAim, tldt = small(), small(), small()
        sgn, kv, jv = small()[:, 0:2], small()[:, 0:24], small()
        maskt = carve(128)
        Bq1, Bq2, Cq1, Cq2 = carve(1024), carve(1024), carve(1024), carve(1024)
        for dst, nm in [(tAre, "Are"), (tAim, "Aim"), (tldt, "ldt")]:
            tr.dma("sp", dst, ssm_in[nm][L], writes=[K_], grp=("setup", L))
        for dst, nm in [(sgn, "sgn"), (kv, "kvals"), (jv, "jvals"), (maskt, "mask_ts")]:
            tr.dma("sp", dst, ssm_in[nm], writes=[K_], grp=("setup", L))
        for dst, nm in [(Bq1, "Bq1"), (Bq2, "Bq2"), (Cq1, "Cq1"), (Cq2, "Cq2")]:
            tr.dma("sp", dst, ssm_in[nm][L], writes=[K_], grp=("setup", L))
        dtt, ar, th, e1, c1, s1, lbr, lbi, nr, den, cr, ci, sci, t64, nsgn = [small() for _ in range(15)]
        nsgn = nsgn[:, 0:1]
        A_(lambda e: e.activation(out=dtt, in_=tldt, func=AF.Exp))
        D_(lambda e: e.tensor_tensor(out=ar, in0=tAre, in1=dtt, op=ALU.mult))
        D_(lambda e: e.tensor_tensor(out=th, in0=tAim, in1=dtt, op=ALU.mult))
        A_(lambda e: e.activation(out=e1, in_=ar, func=AF.Exp))
        A_(lambda e: e.activation(out=r8[:, L, :], in_=ar, func=AF.Exp, scale=8.0))
        sincos(th, c1, s1, t64)
        D_(lambda e: e.tensor_tensor(out=lbr, in0=e1, in1=c1, op=ALU.mult))
        D_(lambda e: e.tensor_tensor(out=lbi, in0=e1, in1=s1, op=ALU.mult))
        D_(lambda e: e.tensor_scalar(out=nr, in0=lbr, scalar1=-1.0, scalar2=None, op0=ALU.add))
        D_(lambda e: e.tensor_tensor(out=den, in0=tAre, in1=tAre, op=ALU.mult))
        D_(lambda e: e.tensor_tensor(out=t64, in0=tAim, in1=tAim, op=ALU.mult))
        D_(lambda e: e.tensor_tensor(out=den, in0=den, in1=t64, op=ALU.add))
        D_(lambda e: e.reciprocal(out=den, in_=den))
        D_(lambda e: e.tensor_tensor(out=cr, in0=nr, in1=tAre, op=ALU.mult))
        D_(lambda e: e.tensor_tensor(out=t64, in0=lbi, in1=tAim, op=ALU.mult))
        D_(lambda e: e.tensor_tensor(out=cr, in0=cr, in1=t64, op=ALU.add))
        D_(lambda e: e.tensor_tensor(out=cr, in0=cr, in1=den, op=ALU.mult))
        D_(lambda e: e.tensor_tensor(out=ci, in0=lbi, in1=tAre, op=ALU.mult))
        D_(lambda e: e.tensor_tensor(out=t64, in0=nr, in1=tAim, op=ALU.mult))
        D_(lambda e: e.tensor_tensor(out=ci, in0=ci, in1=t64, op=ALU.subtract))
        D_(lambda e: e.tensor_tensor(out=ci, in0=ci, in1=den, op=ALU.mult))
        D_(lambda e: e.tensor_scalar(out=sci, in0=ci, scalar1=sgn[:, 0:1], scalar2=None, op0=ALU.mult))
        D_(lambda e: e.tensor_scalar(out=nsgn, in0=sgn[:, 0:1], scalar1=-1.0, scalar2=None, op0=ALU.mult))
        th8f, t64b = small(), small()
        D_(lambda e: e.tensor_scalar(out=th8f, in0=th, scalar1=8.0 / TWO_PI, scalar2=None, op0=ALU.mult))
        D_(lambda e: e.tensor_scalar(out=t64b, in0=th8f, scalar1=MAGIC, scalar2=None, op0=ALU.add))
        D_(lambda e: e.tensor_scalar(out=t64b, in0=t64b, scalar1=MAGIC, scalar2=None, op0=ALU.subtract))
        D_(lambda e: e.tensor_tensor(out=th8f, in0=th8f, in1=t64b, op=ALU.subtract))
        BQ1, BQ2, tB = carve(1024), carve(1024), carve(1024)
        v3 = lambda a: a.rearrange("p (g c) -> p g c", c=16)
        bc = lambda a: a.unsqueeze(2).broadcast_to([128, 64, 16])
        D_(lambda e: e.tensor_tensor(out=v3(BQ1), in0=v3(Bq1), in1=bc(cr), op=ALU.mult))
        D_(lambda e: e.tensor_tensor(out=v3(tB), in0=v3(Bq2), in1=bc(sci), op=ALU.mult))
        D_(lambda e: e.tensor_tensor(out=BQ1, in0=BQ1, in1=tB, op=ALU.add))
        D_(lambda e: e.tensor_tensor(out=v3(BQ2), in0=v3(Bq2), in1=bc(cr), op=ALU.mult))
        D_(lambda e: e.tensor_tensor(out=v3(tB), in0=v3(Bq1), in1=bc(sci), op=ALU.mult))
        D_(lambda e: e.tensor_tensor(out=BQ2, in0=BQ2, in1=tB, op=ALU.subtract))
        D_(lambda e: e.tensor_scalar(out=Cq1[64:128, :], in0=Cq1[64:128, :], scalar1=-1.0, scalar2=None, op0=ALU.mult))
        D_(lambda e: e.tensor_scalar(out=Cq2[64:128, :], in0=Cq2[64:128, :], scalar1=-1.0, scalar2=None, op0=ALU.mult))
        P1, P2 = carve(1536), carve(1536)
        ang, tA, ek, ck, sk = [bigf[:, i * 1536:(i + 1) * 1536] for i in range(5)]
        k3 = lambda a: a.rearrange("p (k g) -> p k g", g=64)
        kb_ = kv.unsqueeze(2).broadcast_to([128, 24, 64])
        gb_ = lambda a: a.unsqueeze(1).broadcast_to([128, 24, 64])
        D_(lambda e: e.tensor_tensor(out=k3(ang), in0=kb_, in1=gb_(th), op=ALU.mult))
        D_(lambda e: e.tensor_tensor(out=k3(ek), in0=kb_, in1=gb_(ar), op=ALU.mult))
        A_(lambda e: e.activation(out=ek, in_=ek, func=AF.Exp))
        sincos(ang, ck, sk, tA)
        D_(lambda e: e.tensor_tensor(out=P1, in0=ek, in1=ck, op=ALU.mult))
        D_(lambda e: e.tensor_tensor(out=P2, in0=ek, in1=sk, op=ALU.mult))
        D_(lambda e: e.tensor_scalar(out=P2, in0=P2, scalar1=sgn[:, 0:1], scalar2=None, op0=ALU.mult))
        D_(lambda e: e.tensor_copy(out=pw4[:, L, 0, :], in_=k3(P1)[:, 3, :]))
        D_(lambda e: e.tensor_copy(out=pw4[:, L, 1, :], in_=k3(P2)[:, 3, :]))
        stage = carve(2048).bitcast(BF16).rearrange("p (k g m) -> p k g m", k=4, g=8)
        wstage = carve(2048).rearrange("p (w g j) -> p w g j", w=2, g=16)
        P1k, P2k = k3(P1), k3(P2)
        for half in range(2):
            g0 = 32 * half
            Cst = xf[:, 0:4608].rearrange("p (g t c) -> p g t c", g=32, t=9)
            tC = bigf[:, 0:4608].rearrange("p (g t c) -> p g t c", g=32, t=9)
            pC = lambda P: P[:, 15:24, g0:g0 + 32].rearrange("p t g -> p g t").unsqueeze(3).broadcast_to([128, 32, 9, 16])
            qC = lambda Q: v3(Q)[:, g0:g0 + 32, :].unsqueeze(2).broadcast_to([128, 32, 9, 16])
            D_(lambda e: e.tensor_tensor(out=Cst, in0=pC(P1k), in1=qC(Cq1), op=ALU.mult))
            D_(lambda e: e.tensor_tensor(out=tC, in0=pC(P2k), in1=qC(Cq2), op=ALU.mult))
            D_(lambda e: e.tensor_tensor(out=Cst, in0=Cst, in1=tC, op=ALU.add))
            Bst = bigf[:, 0:4096].rearrange("p (g s c) -> p g s c", g=32, s=8)
            Sst = bigf[:, 4096:8192].rearrange("p (g s c) -> p g s c", g=32, s=8)
            tS = bigf[:, 8192:12288].rearrange("p (g s c) -> p g s c", g=32, s=8)
            pB = lambda P, i0: P[:, i0:i0 + 8, g0:g0 + 32].rearrange("p s g -> p g s").unsqueeze(3).broadcast_to([128, 32, 8, 16])
            qB = lambda Q: v3(Q)[:, g0:g0 + 32, :].unsqueeze(2).broadcast_to([128, 32, 8, 16])
            for dst, i0 in ((Bst, 7), (Sst, 0)):
                D_(lambda e: e.tensor_tensor(out=dst, in0=pB(P1k, i0), in1=qB(BQ1), op=ALU.mult))
                D_(lambda e: e.tensor_tensor(out=tS, in0=pB(P2k, i0), in1=qB(BQ2), op=ALU.mult))
                D_(lambda e: e.tensor_tensor(out=dst, in0=dst, in1=tS, op=ALU.add))
            for cq in range(4):
                chunk = 4 * half + cq
                for g4 in range(2):
                    for i in range(4):
                        gl = 8 * cq + 4 * g4 + i
                        tr.op("pe", lambda e: e.matmul(bank[0][:, i * 128:(i + 1) * 128],
                                                       lhsT=Bst[:, gl, :, :].rearrange("p s c -> p (s c)"),
                                                       rhs=Cst[:, gl, 0:8, :].rearrange("p t c -> p (t c)"), start=True, stop=True),
                              reads=[K_], writes=[B(0)], inc=(i == 3))
                        tr.op("pe", lambda e: e.transpose(out=bank[1][:, i * 128:(i + 1) * 128],
                                                          in_=Sst[:, gl, :, :].rearrange("p s c -> p (s c)"), identity=ident[:]),
                              reads=[K_, "consts"], writes=[B(1)], inc=(i == 3))
                    gs = slice(4 * g4, 4 * g4 + 4)
                    b0v = bank[0][:, :].rearrange("p (g m) -> p g m", g=4)
                    b1v = bank[1][:, :].rearrange("p (g m) -> p g m", g=4)
                    tr.op("dve", lambda e: e.tensor_tensor(out=stage[:, 0, gs, :], in0=b0v,
                                                           in1=maskt.unsqueeze(1).broadcast_to([128, 4, 128]), op=ALU.mult),
                          reads=[], writes=[B(0), K_])
                    tr.op("act", lambda e: e.activation(out=stage[:, 2, gs, :], in_=b1v, func=AF.Copy), reads=[], writes=[B(1), K_])
                    tr.op("act", lambda e: e.activation(out=stage[:, 3, gs, 0:64], in_=b1v[:, :, 64:128], func=AF.Copy),
                          reads=[], writes=[B(1), K_])
                    tr.op("act", lambda e: e.activation(out=stage[:, 3, gs, 64:128], in_=b1v[:, :, 0:64], func=AF.Copy),
                          reads=[], writes=[B(1), K_])
                D_(lambda e: e.tensor_copy(out=stage[:, 1, :, :],
                                           in_=Cst[:, 8 * cq:8 * cq + 8, 1:9, :].rearrange("p g t c -> p g (t c)")))
                tr.dma("sp", MATS_D[L, chunk], stage[:, :, :, :].rearrange("p k g m -> p (k g m)"), reads=[K_])
            for qtr in range(2):
                gq0 = g0 + 16 * qtr
                angW, cosW, sinW, tAW = [xf[:, 4608 + i * 1024:4608 + (i + 1) * 1024] for i in range(4)]
                w3 = lambda a: a.rearrange("p (g j) -> p g j", j=64)
                D_(lambda e: e.tensor_tensor(out=w3(angW), in0=jv.unsqueeze(1).broadcast_to([128, 16, 64]),
                                             in1=th8f[:, gq0:gq0 + 16].unsqueeze(2).broadcast_to([128, 16, 64]), op=ALU.mult))
                sincos(angW, cosW, sinW, tAW, cycles=True)
                D_(lambda e: e.tensor_copy(out=wstage[:, 0, :, :], in_=w3(cosW)))
                D_(lambda e: e.tensor_scalar(out=wstage[:, 1, :, :], in0=w3(sinW), scalar1=nsgn, scalar2=None, op0=ALU.mult))
                for c2 in range(2):
                    chunk = gq0 // 8 + c2
                    tr.dma("sp", WTAB_D[L, chunk].rearrange("p (w g j) -> p w g j", w=2, g=8),
                           wstage[:, :, 8 * c2:8 * c2 + 8, :], reads=[K_])

    for L in range(DEPTH):
        ssm_setup(L)
    tr.barrier()

    plan = []

    wcol_of = lambda kcin: 256 if kcin * 256 <= SLOTW else 128

    def plan_linear(wd, kcin, nout):
        wv = wd.rearrange("(c p) n -> p c n", p=128)
        wc = wcol_of(kcin)
        for m2 in range(nout // wc):
            def mk(slot, wv=wv, m2=m2, kcin=kcin, wc=wc):
                v = wslot[slot][:, 0:kcin * wc].rearrange("p (c n) -> p c n", c=kcin)
                return [(v[:, c0:min(c0 + 4, kcin), :], wv[:, c0:min(c0 + 4, kcin), m2 * wc:(m2 + 1) * wc], None)
                        for c0 in range(0, kcin, 4)]
            plan.append(mk)

    def plan_gu(wg, wu):
        gv = wg.rearrange("(c p) n -> p c n", p=128)
        uv = wu.rearrange("(c p) n -> p c n", p=128)
        for m2 in range(DFF // 256):
            def mk(slot, gv=gv, uv=uv, m2=m2):
                v = wslot[slot][:, 0:2 * KC * 256].rearrange("p (g c n) -> p g c n", g=2, c=KC)
                out = []
                for gi, sv in enumerate((gv, uv)):
                    for c0 in range(0, KC, 4):
                        out.append((v[:, gi, c0:c0 + 4, :], sv[:, c0:c0 + 4, m2 * 256:(m2 + 1) * 256], gi))
                return out
            plan.append(mk)

    for t in range(NT):
        for L in range(DEPTH):
            plan_gu(W["ffn1_w_gate"][L], W["ffn1_w_up"][L])
            plan_linear(W["ffn1_w_down"][L], FC, D)
            plan_linear(W["w_in_aug"][L], KC, INW)
            plan_linear(W["w_out"][L], KC, D)
            plan_gu(W["ffn2_w_gate"][L], W["ffn2_w_up"][L])
            plan_linear(W["ffn2_w_down"][L], FC, D)
    st = {"issued": 0, "used": 0}

    def ensure(n):
        while st["issued"] < min(n, len(plan)):
            i = st["issued"]
            slot = i % NSLOT
            for dst, src, gi in plan[i](slot):
                if gi is None:
                    keys = [("ws", slot), ("ws", slot, 0), ("ws", slot, 1)]
                else:
                    keys = [("ws", slot, gi)] + ([("ws", slot)] if gi == 0 else [])
                tr.dma("pool", dst, src, writes=keys, grp=("ws", i))
            st["issued"] += 1

    def next_slot():
        i = st["used"]
        st["used"] += 1
        ensure(i + NSLOT)
        return i % NSLOT

    def rmsnorm(src, nch, gname, L, dst, S, n_feat, src_keys, coff=0, doff=0, perm=False):
        W_ = TM + S
        for c in range(nch):
            sq = sqb[c % 2]
            last = (c == nch - 1)
            tr.op("act", lambda e: e.activation(out=sq[:, 0:W_], in_=src[:, coff + c, 0:W_], func=AF.Square),
                  reads=src_keys, writes=[("sqb", c % 2)])
            tr.op("pe", lambda e: e.matmul(bank[6][:, 0:TM], lhsT=ones_b[:], rhs=sq[:, 0:TM], start=(c == 0), stop=last),
                  reads=["ones", ("sqb", c % 2)], writes=[B(6)], inc=(not S))
            if S:
                tr.op("pe", lambda e: e.matmul(bank[7][:, 0:S], lhsT=ones_b[:], rhs=sq[:, TM:TM + S], start=(c == 0), stop=last),
                      reads=[], writes=[B(7)], inc=True)
        tr.op("act", lambda e: e.activation(out=rstd[:, 0:TM], in_=bank[6][:, 0:TM], func=AF.Sqrt, scale=1.0 / n_feat, bias=EPS),
              reads=[], writes=[B(6), "rstd"])
        if S:
            tr.op("act", lambda e: e.activation(out=rstd[:, TM:TM + S], in_=bank[7][:, 0:S], func=AF.Sqrt, scale=1.0 / n_feat, bias=EPS),
                  reads=[], writes=[B(7), "rstd"])
        tr.op("dve", lambda e: e.reciprocal(out=rstd[:, 0:W_], in_=rstd[:, 0:W_]), reads=[], writes=["rstd"])
        for c in range(nch):
            if perm:
                pv = lambda a: a.rearrange("p (s j) -> p s j", s=8)
                tr.op("dve", lambda e: e.scalar_tensor_tensor(out=dst[:, doff + c, 0:TM].rearrange("p (j s) -> p s j", s=8),
                                                              in0=pv(src[:, coff + c, 0:TM]), scalar=gains[gname][:, L, c:c + 1],
                                                              in1=pv(rstd[:, 0:TM]), op0=ALU.mult, op1=ALU.mult),
                      reads=list(src_keys) + ["rstd", ("gain", gname)], writes=[("xn", doff + c)])
                if S:
                    tr.op("dve", lambda e: e.scalar_tensor_tensor(out=dst[:, doff + c, TM:W_], in0=src[:, coff + c, TM:W_],
                                                                  scalar=gains[gname][:, L, c:c + 1],
                                                                  in1=rstd[:, TM:W_], op0=ALU.mult, op1=ALU.mult),
                          reads=list(src_keys) + ["rstd", ("gain", gname)], writes=[("xn", doff + c)])
                continue
            tr.op("dve", lambda e: e.scalar_tensor_tensor(out=dst[:, doff + c, 0:W_], in0=src[:, coff + c, 0:W_],
                                                          scalar=gains[gname][:, L, c:c + 1],
                                                          in1=rstd[:, 0:W_], op0=ALU.mult, op1=ALU.mult),
                  reads=list(src_keys) + ["rstd", ("gain", gname)], writes=[("xn", doff + c)])

    def linear(src, kcin, src_keys, nout, S, evac, wcols=None):
        wc = wcol_of(kcin)
        nj = wc // 128
        for m2 in range(nout // wc):
            slot = next_slot()
            wv = wslot[slot][:, 0:kcin * wc].rearrange("p (c n) -> p c n", c=kcin)
            for j in range(nj):
                m = nj * m2 + j
                pb = m % 2
                for k in range(kcin):
                    last = (k == kcin - 1)
                    lw = wv[:, k, j * 128:(j + 1) * 128]
                    tr.op("pe", lambda e: e.matmul(bank[pb][:], lhsT=lw, rhs=src[:, k, 0:TM], start=(k == 0), stop=last),
                          reads=(list(src_keys) + [("ws", slot)]) if k == 0 else [], writes=[B(pb)], inc=(last and not S))
                    if S:
                        tr.op("pe", lambda e: e.matmul(bank[4 + pb][:, 0:S], lhsT=lw, rhs=src[:, k, TM:TM + S], start=(k == 0), stop=last),
                              reads=[], writes=[B(4 + pb)], inc=last)
                evac(m, pb)

    def ffn(L, which, S):
        pre = f"ffn{which}"
        rmsnorm(x, KC, pre + "_norm", L, xn, S, D, ["x"])
        xn_keys = [("xn", c) for c in range(KC)]
        W_ = TM + S
        for m2 in range(DFF // 256):
            slot = next_slot()
            wv = wslot[slot][:, 0:2 * KC * 256].rearrange("p (g c n) -> p g c n", g=2, c=KC)
            for j in range(2):
                f = 2 * m2 + j
                pb = f % 2
                for gi in range(2):
                    bk = 2 * gi + pb
                    for k in range(KC):
                        last = (k == KC - 1)
                        lw = wv[:, gi, k, j * 128:(j + 1) * 128]
                        tr.op("pe", lambda e: e.matmul(bank[bk][:], lhsT=lw, rhs=xn[:, k, 0:TM], start=(k == 0), stop=last),
                              reads=(xn_keys + [("ws", slot, gi)]) if k == 0 else [], writes=[B(bk)], inc=(last and not S))
                        if S:
                            tr.op("pe", lambda e: e.matmul(bank[4 + pb][:, gi * 64:gi * 64 + S], lhsT=lw, rhs=xn[:, k, TM:TM + S],
                                                           start=(k == 0), stop=last), reads=[], writes=[B(4 + pb)], inc=last)
                tr.op("act", lambda e: e.activation(out=tmp[pb][:, 0:TM], in_=bank[pb][:], func=AF.Silu),
                      reads=[], writes=[B(pb), ("tmp", pb)])
                tr.op("dve", lambda e: e.tensor_tensor(out=h[:, f, 0:TM], in0=tmp[pb][:, 0:TM], in1=bank[2 + pb][:], op=ALU.mult),
                      reads=[("tmp", pb)], writes=[B(2 + pb), ("h", f)])
                if S:
                    tr.op("act", lambda e: e.activation(out=tmp[pb][:, TM:TM + S], in_=bank[4 + pb][:, 0:S], func=AF.Silu),
                          reads=[], writes=[B(4 + pb), ("tmps", pb)])
                    tr.op("dve", lambda e: e.tensor_tensor(out=h[:, f, TM:TM + S], in0=tmp[pb][:, TM:TM + S], in1=bank[4 + pb][:, 64:64 + S],
                                                           op=ALU.mult), reads=[("tmps", pb)], writes=[B(4 + pb), ("h", f)])
        h_keys = [("h", f) for f in range(FC)]

        def evac(d, pb):
            tr.op("dve", lambda e: e.scalar_tensor_tensor(out=x[:, d, 0:TM], in0=bank[pb][:], scalar=0.5, in1=x[:, d, 0:TM],
                                                          op0=ALU.mult, op1=ALU.add), reads=[], writes=[B(pb), "x"])
            if S:
                tr.op("dve", lambda e: e.scalar_tensor_tensor(out=x[:, d, TM:TM + S], in0=bank[4 + pb][:, 0:S], scalar=0.5,
                                                              in1=x[:, d, TM:TM + S], op0=ALU.mult, op1=ALU.add),
                      reads=[], writes=[B(4 + pb), "x"])
        linear(h, FC, h_keys, D, S, evac)

    def attention(L, t, S):
        W_ = TM + S
        last_tile = bool(S)
        zf = big
        kcq = zf[:, 0:1024].bitcast(BF16).rearrange("p (b k s) -> p b k s", b=4, k=4)
        vcq = zf[:, 1024:2048].bitcast(BF16).rearrange("p (b k d) -> p b k d", b=4, k=4)
        knf = zf[:, 2048:2816].rearrange("p (k t) -> p k t", k=4)
        ZU = [("z", c) for c in range(8)]
        AK = ["A_kc", "A_vc", "knf"]
        if last_tile:
            tr.handoff(ZU, AK)
            for q4 in range(4):
                bs_ = slice(4 * q4, 4 * q4 + 4)
                tr.dma("sp", SK_D[L, bs_, 0:124, :], ckraw_d[L, bs_, 4:128, :])
                tr.dma("sp", SV_D[L, bs_, 0:124, :], cvraw_d[L, bs_, 4:128, :])
        for c in range(12):
            isq = c < 8
            sc = 8 + c
            bm, bs = (6, 7) if c % 2 == 0 else (4, 5)
            sq, rs = sqb[c % 2], rsb[c % 2]
            tr.op("act", lambda e: e.activation(out=sq[:, 0:W_], in_=z[:, sc, 0:W_], func=AF.Square),
                  reads=[("z", sc)], writes=[("sqb", c % 2)])
            tr.op("pe", lambda e: e.matmul(bank[bm][:, 0:TM], lhsT=bdones[:], rhs=sq[:, 0:TM], start=True, stop=True),
                  reads=[("sqb", c % 2), "consts"], writes=[B(bm)])
            scl, bia = (1.0, 64 * EPS) if isq else (1.0 / 64, EPS)
            tr.op("act", lambda e: e.activation(out=rs[:, 0:TM], in_=bank[bm][:, 0:TM], func=AF.Sqrt, scale=scl, bias=bia),
                  reads=[], writes=[B(bm), ("tmp", c % 2)])
            if S:
                tr.op("pe", lambda e: e.matmul(bank[bs][:, 0:S], lhsT=bdones[:], rhs=sq[:, TM:TM + S], start=True, stop=True),
                      reads=[("sqb", c % 2), "consts"], writes=[B(bs)])
                tr.op("act", lambda e: e.activation(out=rs[:, TM:TM + S], in_=bank[bs][:, 0:S], func=AF.Sqrt, scale=scl, bias=bia),
                      reads=[], writes=[B(bs), ("tmp", c % 2)])
            tr.op("dve", lambda e: e.reciprocal(out=rs[:, 0:W_], in_=rs[:, 0:W_]), reads=[], writes=[("tmp", c % 2)])
            if isq:
                tr.op("dve", lambda e: e.scalar_tensor_tensor(out=xn[:, c, 0:W_], in0=z[:, sc, 0:W_], scalar=gqk[:, L, 0:1],
                                                              in1=rs[:, 0:W_], op0=ALU.mult, op1=ALU.mult),
                      reads=[("z", sc), ("tmp", c % 2), "consts"], writes=[("xn", c)])
            else:
                kvh_ = c - 8
                for par in range(2):
                    lo, hi = 64 * par, 64 * par + 64
                    oc = 8 + 2 * kvh_ + par
                    tr.op("dve", lambda e: e.memset(xn[64 - lo:128 - lo, oc, 0:W_], 0.0), reads=[], writes=[("xn", oc)])
                    tr.op("dve", lambda e: e.scalar_tensor_tensor(out=xn[lo:hi, oc, 0:W_], in0=z[lo:hi, sc, 0:W_], scalar=gqk[lo:hi, L, 1:2],
                                                                  in1=rs[lo:hi, 0:W_], op0=ALU.mult, op1=ALU.mult),
                          reads=[("z", sc), ("tmp", c % 2), "consts"], writes=[("xn", oc)])
                if last_tile:
                    tr.op("dve", lambda e: e.scalar_tensor_tensor(out=knf[:, kvh_, :], in0=z[:, sc, 384:576], scalar=gqk[:, L, 1:2],
                                                                  in1=rs[:, 384:576], op0=ALU.mult, op1=ALU.mult),
                          reads=[("z", sc), ("tmp", c % 2), "consts"], writes=["knf"])
        ATT = int(os.environ.get('ATT_STAGE', '9'))
        if ATT < 2:
            return
        for b in range(4):
            for vc in range(2):
                tr.op("pe", lambda e: e.transpose(out=bank[6][:, vc * 128:(vc + 1) * 128], in_=z[:, 20 + vc, b * 128:(b + 1) * 128],
                                                  identity=ident[:]),
                      reads=[("z", 20 + vc), "consts"], writes=[B(6)], inc=(vc == 1))
            vt3 = bank[6][:, 0:256].rearrange("p (k d) -> p k d", k=4)
            tr.op("act", lambda e: e.activation(out=vtok[:, b, :, 0:64], in_=vt3, func=AF.Copy), reads=[], writes=[B(6), ("vtok", b)])
            tr.op("act", lambda e: e.activation(out=vtok[:, b, :, 64:128], in_=vt3, func=AF.Copy), reads=[], writes=[B(6), ("vtok", b)])
            if last_tile and b == 3:
                tr.op("act", lambda e: e.activation(out=rec[1][:], in_=bank[6][:, 0:256], func=AF.Copy), reads=[], writes=[B(6), ("rec", 1)])
                tr.dma("sp", PV_D[L], rec[1][:], reads=[("rec", 1)])
        if last_tile:
            for kvh in range(KVH):
                tr.op("pe", lambda e: e.transpose(out=bank[6][:, kvh * 128:(kvh + 1) * 128], in_=knf[:, kvh, 0:128], identity=ident[:]),
                      reads=["knf", "consts"], writes=[B(6)], inc=(kvh == 3))
            tr.op("act", lambda e: e.activation(out=rec[0][:].rearrange("p (k d) -> p k d", k=4),
                                                in_=bank[6][:].rearrange("p (k d) -> p k d", k=4)[:, :, 0:64], func=AF.Copy),
                  reads=[], writes=[B(6), ("rec", 0)])
            tr.dma("sp", PK_D[L], rec[0][:], reads=[("rec", 0)])
        items = []
        for b in range(4):
            for kvh in range(KVH):
                kcur = [("xn", 8 + 2 * kvh), ("xn", 9 + 2 * kvh)]
                kbs = []
                if b > 0:
                    kbs.append((lambda par, b=b, kvh=kvh: xn[:, 8 + 2 * kvh + par, (b - 1) * 128:b * 128],
                                vtok[:, b - 1, kvh, :], kcur, [("vtok", b - 1)], 0))
                elif t > 0:
                    kbs.append((lambda par, kvh=kvh: kprev[L][:, 2 * kvh + par, :], vprev[L][:, kvh, :], [("kprev", L)], [("vprev", L)], 0))
                kbs.append((lambda par, b=b, kvh=kvh: xn[:, 8 + 2 * kvh + par, b * 128:(b + 1) * 128],
                            vtok[:, b, kvh, :], kcur, [("vtok", b)], 1))
                for ki, kb in enumerate(kbs):
                    items.append((b, kvh, ki, len(kbs), kb, b * KVH + kvh))

        def emit_scores(idx):
            b, kvh, ki, nk, (kfn, vap, kkeys, vkeys, ei), grp_ = items[idx]
            sbk = idx % 4
            for par in range(2):
                tr.op("pe", lambda e: e.matmul(bank[sbk][:, par * 256:(par + 1) * 256], lhsT=kfn(par),
                                               rhs=xn[:, 2 * kvh:2 * kvh + 2, b * 128:(b + 1) * 128], start=True, stop=False),
                      reads=kkeys + [("xn", 2 * kvh), ("xn", 2 * kvh + 1)], writes=[B(sbk)], inc=False)
                tr.op("pe", lambda e: e.matmul(bank[sbk][:, par * 256:(par + 1) * 256], lhsT=identb[:],
                                               rhs=Etab[:, ei, kvh, par * 256:(par + 1) * 256], start=False, stop=True),
                      reads=["consts"], writes=[B(sbk)], inc=(par == 1))

        def emit_rest(idx):
            b, kvh, ki, nk, (kfn, vap, kkeys, vkeys, ei), grp_ = items[idx]
            q0 = b * 128
            sbk = idx % 4
            pt, pk = pT[idx % 2], ("pT", idx % 2)
            ob, db, ri = 4 + 2 * (grp_ % 2), 5 + 2 * (grp_ % 2), grp_ % 2
            first, lastk = (ki == 0), (ki == nk - 1)
            tr.op("act", lambda e: e.activation(out=pt[:], in_=bank[sbk][:], func=AF.Exp), reads=[], writes=[B(sbk), pk])
            tr.op("pe", lambda e: e.matmul(bank[ob][:], lhsT=vap, rhs=pt[:], start=first, stop=lastk),
                  reads=vkeys + [pk], writes=[B(ob)], inc=False)
            tr.op("pe", lambda e: e.matmul(bank[db][:], lhsT=ones_b[:], rhs=pt[:], start=first, stop=lastk),
                  reads=["ones"], writes=[B(db)], inc=True)
            if not lastk:
                return
            for par in range(2):
                rows = slice(64 * par, 64 * par + 64)
                for c2 in range(2):
                    tr.op("dve", lambda e: e.tensor_scalar(out=rec[ri][rows, c2 * 128:(c2 + 1) * 128],
                                                           in0=bank[db][rows, par * 256 + c2 * 128:par * 256 + (c2 + 1) * 128],
                                                           scalar1=sinkexp[rows, L, kvh, c2:c2 + 1], scalar2=None, op0=ALU.add),
                          reads=["consts"], writes=[B(db), ("rec", ri)])
            tr.op("dve", lambda e: e.reciprocal(out=rec[ri][:], in_=rec[ri][:]), reads=[], writes=[("rec", ri)])
            for par in range(2):
                rows = slice(64 * par, 64 * par + 64)
                tr.op("dve", lambda e: e.tensor_tensor(out=z[rows, 8 + 2 * kvh:10 + 2 * kvh, q0:q0 + 128],
                                                       in0=bank[ob][rows, par * 256:(par + 1) * 256].rearrange("p (c q) -> p c q", c=2),
                                                       in1=rec[ri][rows, :].rearrange("p (c q) -> p c q", c=2), op=ALU.mult),
                      reads=[("rec", ri)], writes=[B(ob), ("z", 8 + 2 * kvh), ("z", 9 + 2 * kvh)])

        emit_scores(0)
        for idx in range(len(items)):
            if idx + 1 < len(items):
                emit_scores(idx + 1)
            emit_rest(idx)
        tr.op("dve", lambda e: e.tensor_copy(out=kprev[L][:], in_=xn[:, 8:16, 384:512]),
              reads=[("xn", 8 + k) for k in range(8)], writes=[("kprev", L)])
        tr.op("dve", lambda e: e.tensor_copy(out=vprev[L][:], in_=vtok[:, 3, :, :]), reads=[("vtok", 3)], writes=[("vprev", L)])
        if not last_tile:
            return
        SC = slice(TM, TM + S)
        XQ = [("xn", c) for c in range(8)]
        XKZ = [("xn", 8 + c) for c in range(8)]
        for kvh in range(KVH):
            tr.op("pe", lambda e: e.transpose(out=bank[6][:, kvh * 128:(kvh + 1) * 128], in_=knf[:, kvh, 64:192], identity=ident[:]),
                  reads=["knf", "consts"], writes=[B(6)], inc=(kvh == 3))
        tr.op("act", lambda e: e.activation(out=rec[0][64:128, :].rearrange("p (k d) -> p k d", k=4),
                                            in_=bank[6][64:128, :].rearrange("p (k d) -> p k d", k=4)[:, :, 0:64], func=AF.Copy),
              reads=[], writes=[B(6), ("rec", 0)])
        for i_ in range(LS):
            tr.dma("sp", SK_D[L, :, 124 + i_, :], rec[0][64 + 16 * i_:64 + 16 * i_ + 16, :], reads=[("rec", 0)])
        for vc in range(2):
            tr.op("pe", lambda e: e.transpose(out=bank[7][:, vc * 128:(vc + 1) * 128], in_=z[:, 20 + vc, 448:576], identity=ident[:]),
                  reads=[("z", 20 + vc), "consts"], writes=[B(7)], inc=(vc == 1))
        vs3 = bank[7][64:128, 0:256].rearrange("p (k d) -> p k d", k=4)
        tr.op("act", lambda e: e.activation(out=vsdup[64:128, :, 0:64], in_=vs3, func=AF.Copy), reads=[], writes=[B(7), "vsdup"])
        tr.op("act", lambda e: e.activation(out=vsdup[64:128, :, 64:128], in_=vs3, func=AF.Copy), reads=[], writes=[B(7), "vsdup"])
        tr.op("act", lambda e: e.activation(out=rec[1][64:128, :], in_=bank[7][64:128, 0:256], func=AF.Copy), reads=[], writes=[B(7), ("rec", 1)])
        for i_ in range(LS):
            tr.dma("sp", SV_D[L, :, 124 + i_, :], rec[1][64 + 16 * i_:64 + 16 * i_ + 16, :], reads=[("rec", 1)])
        for par in range(2):
            rows = slice(64 * par, 64 * par + 64)
            tr.op("dve", lambda e: e.tensor_copy(out=qzs[rows, par, :, :], in_=xn[rows, 0:8, SC]), reads=XQ, writes=["qzs"])
        for kvh in range(KVH):
            for par in range(2):
                o_ = ((kvh % 2) * 2 + par) * 128
                tr.op("pe", lambda e: e.matmul(bank[kvh // 2][:, o_:o_ + 128], lhsT=xn[:, 8 + 2 * kvh + par, 448:576],
                                               rhs=xn[:, 2 * kvh:2 * kvh + 2, SC], start=True, stop=True),
                      reads=XQ + XKZ, writes=[B(kvh // 2)])
        for hb in range(2):
            tr.op("act", lambda e: e.activation(out=pT[hb][:], in_=bank[hb][:], func=AF.Exp), reads=[], writes=[B(hb), ("pT", hb)])
            tr.op("dve", lambda e: e.tensor_tensor(out=pT[hb][:], in0=pT[hb][:], in1=Entab[:, hb, :], op=ALU.mult),
                  reads=["consts"], writes=[("pT", hb)])
        pts = sqb[0][:, 0:256]
        for q4 in range(4):
            b0 = 4 * q4
            for hh in range(2):
                tr.dma("pool", kcq[:, 2 * hh:2 * hh + 2, :, :],
                       kcd_d[L].rearrange("p (b n) -> p b n", b=NSEQ_S)[:, b0 + 2 * hh:b0 + 2 * hh + 2, :].rearrange("p b (k s) -> p b k s", k=4),
                       writes=["A_kc"], grp=("kc", L, q4))
                tr.dma("pool", vcq[:, 2 * hh:2 * hh + 2, :, :],
                       vcd_d[L].rearrange("p (b n) -> p b n", b=NSEQ_S)[:, b0 + 2 * hh:b0 + 2 * hh + 2, :].rearrange("p b (k s) -> p b k s", k=4),
                       writes=["A_vc"], grp=("vc", L, q4))
            for b4 in range(4):
                for kvh in range(KVH):
                    for par in range(2):
                        col = b4 * 64 + kvh * 16 + par * 8
                        rq = qzs[:, par, 2 * kvh:2 * kvh + 2, :].rearrange("p c (i b) -> p c i b", b=NSEQ_S)[:, :, :, b0 + b4]
                        tr.op("pe", lambda e: e.matmul(bank[2][:, col:col + 8], lhsT=kcq[:, b4, kvh, :], rhs=rq, start=True, stop=True),
                              reads=["A_kc", "qzs"], writes=[B(2)], inc=(b4 == 3 and kvh == 3 and par == 1))
            tr.op("act", lambda e: e.activation(out=pts, in_=bank[2][:, 0:256], func=AF.Exp), reads=[], writes=[B(2), ("sqb", 0)])
            ep = Esamp[:, :, :].unsqueeze(1).broadcast_to([128, 4, 16, 4])
            tr.op("dve", lambda e: e.tensor_tensor(out=pts.rearrange("p (b h i) -> p b h i", b=4, h=16), in0=pts.rearrange("p (b h i) -> p b h i", b=4, h=16),
                                                   in1=ep, op=ALU.mult), reads=["consts"], writes=[("sqb", 0)])
            for b4 in range(4):
                for kvh in range(KVH):
                    col = b4 * 64 + kvh * 16
                    pn = pT[kvh // 2][:, (kvh % 2) * 256:(kvh % 2) * 256 + 256].rearrange("p (h i b) -> p h i b", h=4, i=4)[:, :, :, b0 + b4]
                    lastg = (b4 == 3 and kvh == 3)
                    tr.op("pe", lambda e: e.matmul(bank[4][:, col:col + 16], lhsT=vcq[:, b4, kvh, :], rhs=pts[:, col:col + 16],
                                                   start=True, stop=False), reads=["A_vc", ("sqb", 0)], writes=[B(4)], inc=False)
                    tr.op("pe", lambda e: e.matmul(bank[4][:, col:col + 16], lhsT=vsdup[:, kvh, :], rhs=pn, start=False, stop=True),
                          reads=["vsdup", ("pT", 0), ("pT", 1)], writes=[B(4)], inc=False)
                    tr.op("pe", lambda e: e.matmul(bank[5][:, col:col + 16], lhsT=ones_b[:], rhs=pts[:, col:col + 16],
                                                   start=True, stop=False), reads=["ones"], writes=[B(5)], inc=False)
                    tr.op("pe", lambda e: e.matmul(bank[5][:, col:col + 16], lhsT=ones_b[:], rhs=pn, start=False, stop=True),
                          reads=[], writes=[B(5)], inc=lastg)
            rs_ = rec[0][:, 0:256] if q4 % 2 == 0 else rec[1][:, 0:256]
            rk = ("rec", q4 % 2)
            l5 = lambda a: a.rearrange("p (b k r c i) -> p b k r c i", b=4, k=4, r=2, c=2)
            for kvh in range(KVH):
                for c2 in range(2):
                    tr.op("dve", lambda e: e.tensor_scalar(out=l5(rs_)[:, :, kvh, :, c2, :], in0=l5(bank[5][:, 0:256])[:, :, kvh, :, c2, :],
                                                           scalar1=sinkexp[:, L, kvh, c2:c2 + 1], scalar2=None, op0=ALU.add),
                          reads=["consts"], writes=[B(5), rk])
            tr.op("dve", lambda e: e.reciprocal(out=rs_, in_=rs_), reads=[], writes=[rk])
            for par in range(2):
                rows = slice(64 * par, 64 * par + 64)
                for kvh in range(KVH):
                    for c2 in range(2):
                        zo = z[rows, 8 + 2 * kvh + c2, SC].rearrange("p (i b) -> p b i", b=NSEQ_S)[:, b0:b0 + 4, :]
                        tr.op("dve", lambda e: e.tensor_tensor(out=zo, in0=l5(bank[4][rows, 0:256])[:, :, kvh, par, c2, :],
                                                               in1=l5(rs_[rows, :])[:, :, kvh, par, c2, :], op=ALU.mult),
                              reads=[rk], writes=[B(4), ("z", 8 + 2 * kvh + c2)])
        tr.handoff(AK, ZU)

    def ssm(L, t, S):
        W_ = TM + S
        xnf = xn[:, :, :].rearrange("p c t -> p (c t)")
        mats = xnf[:, 0:4096].rearrange("p (k g m) -> p k g m", k=4, g=8)
        matsf = xnf[:, 0:4096]
        wtab = xnf[:, 4096:6144].bitcast(F32).rearrange("p (w n) -> p w n", w=2)
        Vr = xnf[:, 6144:7168].bitcast(F32)
        Vrs = xnf[:, 7168:8192].bitcast(F32)
        XN = [("xn", c) for c in range(KC)]
        SK = ["S_matsA", ("S_matsB", 0), ("S_wtab", 0), "S_Vr", "S_Vrs"]
        tr.handoff(XN, SK)
        ZD = [("z", c) for c in range(16, 22)]
        SK2 = [("S_matsB", 1), ("S_wtab", 1), "S_A8"]
        tr.handoff(ZD, SK2)
        A8 = z[:, 16, 0:512]
        matsB2 = z[:, 17:19, :].rearrange("p c t -> p (c t)")[:, 0:1024].bitcast(BF16).rearrange("p (k g m) -> p k g m", k=2, g=8)
        wtab2 = z[:, 19:21, :].rearrange("p c t -> p (c t)")[:, 0:1024].rearrange("p (w n) -> p w n", w=2)
        matsBv = [mats[:, 2:4, :, :], matsB2]
        wtabv = [wtab, wtab2]
        tA_ = tmp[0][:, 0:512]
        Yc = tmp[1][:, 0:512]
        Hb = sqb[1][:, 0:512].rearrange("p (g j) -> p g j", g=8)
        h0c = pT[0][:, :].bitcast(F32).rearrange("p (a g b) -> p a g b", a=2, g=8)
        hsn = pT[1][:, 0:256].bitcast(F32).rearrange("p (g b) -> p g b", g=8)
        Ycs = pT[1][:, 256:512].bitcast(F32)
        Hfull = rstd[:, 0:520].rearrange("p (g j) -> p g j", g=8)
        def restack_x(c):
            tr.dma("sp", UB_D[c], ub[:, c, 0:TM], reads=[("ub", c)], writes=[("UBD", c)])
            ubv = UB_D[c].rearrange("(g k) (s j) -> s k g j", k=16, s=8)
            for s_ in range(8):
                tr.dma("sp", Xc[c % 2][16 * s_:16 * s_ + 16, :, :], ubv[s_], reads=[("UBD", c)], writes=[("Xc", c % 2)], grp=("X", L, t, c))
        restack_x(0)
        for c in range(8):
            X = Xc[c % 2]
            if c + 1 < 8:
                restack_x(c + 1)
            pb_ = c % 2
            mB, wt_ = matsBv[pb_], wtabv[pb_]
            KB_, KW_ = ("S_matsB", pb_), ("S_wtab", pb_)
            tr.dma("act", mB, MATS_D[L, c][:, 2048:4096].rearrange("p (k g m) -> p k g m", k=2, g=8), writes=[KB_])
            tr.dma("act", wt_, WTAB_D[L, c].rearrange("p (w n) -> p w n", w=2), writes=[KW_])
            tr.dma("act", matsf[:, 0:2048], MATS_D[L, c][:, 0:2048], writes=["S_matsA"])
            for kk, bk in ((0, 0), (1, 1)):
                for g8 in range(8):
                    tr.op("pe", lambda e: e.matmul(bank[bk][:, g8 * 64:(g8 + 1) * 64], lhsT=mB[:, kk, g8, :], rhs=X[:, g8, :],
                                                   start=True, stop=True),
                          reads=[KB_, ("Xc", c % 2)], writes=[B(bk)], inc=(g8 == 7))
            Wa, Wb = wt_[:, 0, :], wt_[:, 1, :]
            DV = lambda fn, rd, wr: tr.op("dve", fn, reads=rd, writes=wr)
            tA2 = tmp[1][:, 0:512]
            gsl = slice(8 * c, 8 * c + 8)
            v3 = lambda a: a.rearrange("p (g j) -> p g j", g=8)
            l3 = lambda a: v3(a)[:, :, 63]
            DV(lambda e: e.tensor_tensor(out=Vr, in0=bank[0][:], in1=Wa, op=ALU.mult), [KW_], [B(0), "S_Vr"])
            DV(lambda e: e.tensor_tensor(out=tA_, in0=bank[1][:], in1=Wb, op=ALU.mult), [KW_], [B(1), ("tmp", 0)])
            DV(lambda e: e.tensor_tensor(out=Vrs, in0=bank[1][:], in1=Wa, op=ALU.mult), [KW_], [B(1), "S_Vrs"])
            DV(lambda e: e.tensor_tensor(out=tA2, in0=bank[0][:], in1=Wb, op=ALU.mult), [KW_], [B(0), ("tmp", 1)])
            DV(lambda e: e.tensor_copy(out=v3(A8), in_=r8[:, L, gsl].unsqueeze(2).broadcast_to([128, 8, 64])), ["consts"], ["S_A8"])
            DV(lambda e: e.tensor_tensor(out=Vr, in0=Vr, in1=tA_, op=ALU.add), [("tmp", 0)], ["S_Vr"])
            DV(lambda e: e.tensor_tensor(out=Vrs, in0=Vrs, in1=tA2, op=ALU.subtract), [("tmp", 1)], ["S_Vrs"])
            DV(lambda e: e.memset(v3(A8)[:, :, 0], 0.0), [], ["S_A8"])
            DV(lambda e: e.tensor_tensor(out=rec[0][:, 0:8], in0=hin[L][:, gsl], in1=r8[:, L, gsl], op=ALU.mult), [("hin", L), "consts"], [("rec", 0)])
            DV(lambda e: e.tensor_tensor(out=rec[1][:, 0:8], in0=hinsw[L][:, gsl], in1=r8[:, L, gsl], op=ALU.mult), [("hin", L), "consts"], [("rec", 1)])
            DV(lambda e: e.tensor_copy(out=Hfull[:, :, 0], in_=hin[L][:, gsl]), [("hin", L)], ["rstd"])
            DV(lambda e: e.tensor_tensor(out=v3(Vr)[:, :, 0], in0=v3(Vr)[:, :, 0], in1=rec[0][:, 0:8], op=ALU.add), [("rec", 0)], ["S_Vr"])
            DV(lambda e: e.tensor_tensor(out=v3(Vrs)[:, :, 0], in0=v3(Vrs)[:, :, 0], in1=rec[1][:, 0:8], op=ALU.add), [("rec", 1)], ["S_Vrs"])
            DV(lambda e: e.tensor_tensor_scan(out=Vr, data0=A8, data1=Vr, initial=0.0, op0=ALU.mult, op1=ALU.add), ["S_A8"], ["S_Vr"])
            DV(lambda e: e.tensor_tensor_scan(out=Vrs, data0=A8, data1=Vrs, initial=0.0, op0=ALU.mult, op1=ALU.add), ["S_A8"], ["S_Vrs"])
            DV(lambda e: e.tensor_tensor(out=Hfull[:, :, 1:65], in0=v3(Vr), in1=v3(Wa), op=ALU.mult), ["S_Vr", KW_], ["rstd"])
            DV(lambda e: e.tensor_tensor(out=tA_, in0=Vrs, in1=Wb, op=ALU.mult), ["S_Vrs", KW_], [("tmp", 0)])
            DV(lambda e: e.tensor_tensor(out=hinsw[L][:, gsl], in0=l3(Vrs), in1=l3(Wa), op=ALU.mult), ["S_Vrs", KW_], [("hin", L)])
            DV(lambda e: e.tensor_tensor(out=rec[0][:, 0:8], in0=l3(Vr), in1=l3(Wb), op=ALU.mult), ["S_Vr", KW_], [("rec", 0)])
            DV(lambda e: e.tensor_tensor(out=Hfull[:, :, 1:65], in0=Hfull[:, :, 1:65], in1=v3(tA_), op=ALU.subtract), [("tmp", 0)], ["rstd"])
            DV(lambda e: e.tensor_tensor(out=hinsw[L][:, gsl], in0=hinsw[L][:, gsl], in1=rec[0][:, 0:8], op=ALU.add), [("rec", 0)], [("hin", L)])
            DV(lambda e: e.tensor_copy(out=hin[L][:, gsl], in_=Hfull[:, :, 64]), ["rstd"], [("hin", L)])
            DV(lambda e: e.tensor_copy(out=Hb[:], in_=Hfull[:, :, 0:64]), ["rstd"], [("sqb", 1)])
            for g8 in range(8):
                tr.op("pe", lambda e: e.matmul(bank[2][:, g8 * 64:(g8 + 1) * 64], lhsT=mats[:, 0, g8, :], rhs=X[:, g8, :],
                                               start=True, stop=False), reads=["S_matsA", ("Xc", c % 2)], writes=[B(2)], inc=False)
                tr.op("pe", lambda e: e.matmul(bank[2][:, g8 * 64:(g8 + 1) * 64], lhsT=mats[:, 1, g8, :], rhs=Hb[:, g8, :],
                                               start=False, stop=True), reads=[("sqb", 1)], writes=[B(2)], inc=(g8 == 7))
            tr.op("act", lambda e: e.activation(out=Yc[:], in_=bank[2][:], func=AF.Copy), reads=[], writes=[B(2), ("tmp", 1)])
            ydv = YD[c].rearrange("(g k) (t j) -> t k g j", k=16, t=8)
            for t_ in range(8):
                tr.dma("sp", ydv[t_], Yc[16 * t_:16 * t_ + 16, :].rearrange("p (g j) -> p g j", g=8), reads=[("tmp", 1)], writes=[("YD", c)], grp=("Y", L, t, c))
            tr.dma("sp", z[:, c, 0:TM], YD[c], reads=[("YD", c)], writes=[("z", c)])
            if S:
                tr.dma("sp", UBS_D[c], ub[:, c, TM:TM + S], reads=[("ub", c)], writes=[("UBSD", c)])
                usv = UBS_D[c].rearrange("(g k) (i b) -> i k g b", k=16, i=4)
                for i_ in range(4):
                    tr.dma("sp", Xs[64 + 16 * i_:64 + 16 * i_ + 16, :, :], usv[i_], reads=[("UBSD", c)], writes=["Xs"], grp=("Xs", L, c))
                hv = lambda d: d[L].rearrange("p (g b) -> p g b", b=NSEQ_S)[:, gsl, :]
                tr.dma("sp", h0c[:, 0, :, :], hv(h0_d), writes=[("pT", 0)], grp=("h0", L, c))
                tr.dma("sp", h0c[:, 1, :, :], hv(h0sw_d), writes=[("pT", 0)], grp=("h0", L, c))
                DV(lambda e: e.tensor_copy(out=h0b[:], in_=h0c[:, 0, :, :]), [("pT", 0)], ["h0b"])
                for g8 in range(8):
                    tr.op("pe", lambda e: e.matmul(bank[3][:, g8 * 16:(g8 + 1) * 16], lhsT=mB[:, 0, g8, :], rhs=Xs[:, g8, :],
                                                   start=True, stop=True), reads=["S_matsA", KB_, "Xs"], writes=[B(3)], inc=False)
                for g8 in range(8):
                    o_ = 128 + g8 * 16
                    rsh = matsf[:, (8 + g8) * 128 - 64:(8 + g8) * 128 + 64]
                    tr.op("pe", lambda e: e.matmul(bank[3][:, o_:o_ + 16], lhsT=mats[:, 0, g8, :], rhs=Xs[:, g8, :],
                                                   start=True, stop=False), reads=[], writes=[B(3)], inc=False)
                    tr.op("pe", lambda e: e.matmul(bank[3][:, o_:o_ + 16], lhsT=rsh, rhs=h0b[:, g8, :],
                                                   start=False, stop=True), reads=["h0b"], writes=[B(3)], inc=(g8 == 7))
                pw = lambda i: pw4[:, L, i, gsl].unsqueeze(2).broadcast_to([128, 8, 16])
                DV(lambda e: e.tensor_tensor(out=hsn[:], in0=h0c[:, 0, :, :], in1=pw(0), op=ALU.mult), [("pT", 0), "consts"], [("pT", 1)])
                DV(lambda e: e.tensor_tensor(out=h0c[:, 0, :, :], in0=h0c[:, 1, :, :], in1=pw(1), op=ALU.mult), ["h0b", "consts"], [("pT", 0)])
                DV(lambda e: e.tensor_tensor(out=hsn[:], in0=hsn[:], in1=h0c[:, 0, :, :], op=ALU.add), [], [("pT", 1), ("pT", 0)])
                DV(lambda e: e.tensor_tensor(out=hsn[:], in0=hsn[:], in1=bank[3][:, 0:128].rearrange("p (g b) -> p g b", g=8), op=ALU.add),
                   [], [B(3), ("pT", 1)])
                tr.dma("sp", HS_D[L].rearrange("p (g b) -> p g b", b=NSEQ_S)[:, gsl, :], hsn[:], reads=[("pT", 1)])
                tr.op("act", lambda e: e.activation(out=Ycs[64:128, :], in_=bank[3][64:128, 128:256], func=AF.Copy),
                      reads=[], writes=[B(3), ("pT", 1)])
                ysv = YDS[c].rearrange("(g k) (i b) -> i k g b", k=16, i=4)
                for i_ in range(4):
                    tr.dma("sp", ysv[i_], Ycs[64 + 16 * i_:64 + 16 * i_ + 16, :].rearrange("p (g b) -> p g b", g=8),
                           reads=[("pT", 1)], writes=[("YDS", c)], grp=("Ys", L, c))
                tr.dma("sp", z[:, c, TM:TM + S], YDS[c], reads=[("YDS", c)], writes=[("z", c)])
        tr.handoff(SK, XN)
        tr.handoff(SK2, ZD)
        for c in range(8):
            y = z[:, c, 0:W_]
            t1, sg, ygb = tmp[0][:, 0:W_], tmp[1][:, 0:W_], sqb[0][:, 0:W_]
            DV = lambda fn, rd, wr: tr.op("dve", fn, reads=rd, writes=wr)
            DV(lambda e: e.scalar_tensor_tensor(out=y, in0=ub[:, c, 0:W_], scalar=Dv[:, L, c:c + 1], in1=y, op0=ALU.mult, op1=ALU.add),
               [("ub", c), "consts"], [("z", c)])
            DV(lambda e: e.tensor_tensor(out=t1, in0=y, in1=y, op=ALU.mult), [("z", c)], [("tmp", 0)])
            DV(lambda e: e.tensor_scalar(out=t1, in0=t1, scalar1=0.044715, scalar2=1.0, op0=ALU.mult, op1=ALU.add), [], [("tmp", 0)])
            DV(lambda e: e.tensor_tensor(out=t1, in0=t1, in1=y, op=ALU.mult), [("z", c)], [("tmp", 0)])
            tr.op("act", lambda e: e.activation(out=t1, in_=t1, func=AF.Sigmoid, scale=1.5957691216057308), reads=[], writes=[("tmp", 0)])
            DV(lambda e: e.tensor_tensor(out=y, in0=y, in1=t1, op=ALU.mult), [("tmp", 0)], [("z", c)])
            DV(lambda e: e.tensor_copy(out=ygb, in_=y), [("z", c)], [("sqb", 0)])
            tr.op("pe", lambda e: e.matmul(bank[6][:, 0:TM], lhsT=wglu[:, L, c, :], rhs=ygb[:, 0:TM], start=True, stop=True),
                  reads=[("sqb", 0), "consts"], writes=[B(6)])
            tr.op("act", lambda e: e.activation(out=sg[:, 0:TM], in_=bank[6][:, 0:TM], func=AF.Sigmoid), reads=[], writes=[B(6), ("tmp", 1)])
            if S:
                tr.op("pe", lambda e: e.matmul(bank[7][:, 0:S], lhsT=wglu[:, L, c, :], rhs=ygb[:, TM:W_], start=True, stop=True),
                      reads=[("sqb", 0), "consts"], writes=[B(7)])
                tr.op("act", lambda e: e.activation(out=sg[:, TM:W_], in_=bank[7][:, 0:S], func=AF.Sigmoid), reads=[], writes=[B(7), ("tmp", 1)])
            DV(lambda e: e.tensor_tensor(out=y, in0=y, in1=sg, op=ALU.mult), [("tmp", 1)], [("z", c)])

    def mixer(L, t, S):
        W_ = TM + S
        rmsnorm(x, KC, "mix_norm", L, xn, S, D, ["x"])
        xn_keys = [("xn", c) for c in range(KC)]

        def evac_z(m, pb):
            if m < 8:
                tr.op("act", lambda e: e.activation(out=ub[:, m, 0:TM].rearrange("p (s j) -> p j s", s=8),
                                                    in_=bank[pb][:].rearrange("p (j s) -> p j s", s=8), func=AF.Copy),
                      reads=[], writes=[B(pb), ("ub", m)])
                if S:
                    tr.op("act", lambda e: e.activation(out=ub[:, m, TM:TM + S], in_=bank[4 + pb][:, 0:S], func=AF.Copy),
                          reads=[], writes=[B(4 + pb), ("ub", m)])
                return
            tr.op("act", lambda e: e.activation(out=z[:, m, 0:TM], in_=bank[pb][:], func=AF.Copy), reads=[], writes=[B(pb), ("z", m)])
            if S:
                tr.op("act", lambda e: e.activation(out=z[:, m, TM:TM + S], in_=bank[4 + pb][:, 0:S], func=AF.Copy),
                      reads=[], writes=[B(4 + pb), ("z", m)])
        linear(xn, KC, xn_keys, INW, S, evac_z)
        attention(L, t, S)
        ssm(L, t, S)
        zk = [("z", m) for m in range(ZC)]
        rmsnorm(z, 8, "ssm_out_norm", L, xn, S, 1024, zk, coff=0, doff=0, perm=True)
        rmsnorm(z, 8, "attn_out_norm", L, xn, S, 1024, zk, coff=8, doff=8)
        mk = [("xn", c) for c in range(KC)]

        def evac_o(d, pb):
            tr.op("dve", lambda e: e.tensor_tensor(out=x[:, d, 0:TM], in0=bank[pb][:], in1=x[:, d, 0:TM], op=ALU.add),
                  reads=[], writes=[B(pb), "x"])
            if S:
                tr.op("dve", lambda e: e.tensor_tensor(out=x[:, d, TM:TM + S], in0=bank[4 + pb][:, 0:S], in1=x[:, d, TM:TM + S], op=ALU.add),
                      reads=[], writes=[B(4 + pb), "x"])
        linear(xn, KC, mk, D, S, evac_o)

    xv = xTp.rearrange("(c p) t -> p c t", p=128)
    yv = yTp.rearrange("(c p) t -> p c t", p=128)
    for t in range(NT):
        S = TS if t == NT - 1 else 0
        for c0 in range(0, KC, 4):
            tr.dma("sp", x[:, c0:c0 + 4, 0:TM], xv[:, c0:c0 + 4, t * TM:(t + 1) * TM], writes=["x"], grp=("x", t))
        if S:
            tr.dma("sp", x[:, :, TM:TM + S], xTs.rearrange("(c p) t -> p c t", p=128), writes=["x"], grp=("x", t))
        for L in range(DEPTH):
            ffn(L, 1, S)
            mixer(L, t, S)
            ffn(L, 2, S)
        for c0 in range(0, KC, 4):
            tr.dma("sp", yv[:, c0:c0 + 4, t * TM:(t + 1) * TM], x[:, c0:c0 + 4, 0:TM], reads=["x"])
        if S:
            tr.dma("sp", yTs.rearrange("(c p) t -> p c t", p=128), x[:, :, TM:TM + S], reads=["x"])
    for L in range(DEPTH):
        tr.dma("sp", HP_D[L], hin[L][:], reads=[("hin", L)])
    tr.finish()
    return nc


def host_consts():
    h = np.arange(NH, dtype=np.float64)
    slopes = 2.0 ** (-8.0 * (h + 1) / NH)
    sidx = np.arange(128)[:, None]
    qidx = np.arange(128)[None, :]
    E = np.zeros((128, 2, 4, 2, 2, 128), np.float64)
    Es = np.zeros((128, 4, 2, 2, 4), np.float64)
    for kvh in range(4):
        for par in range(2):
            for c2 in range(2):
                m = slopes[4 * kvh + 2 * c2 + par]
                E[:, 1, kvh, par, c2, :] = np.where(sidx <= qidx, -m * (qidx - sidx), -30000.0)
                E[:, 0, kvh, par, c2, :] = np.where(sidx > qidx, -m * (qidx + 128 - sidx), -30000.0)
                Es[:, kvh, par, c2, :] = np.where(sidx > qidx[:, :4], np.exp(-m * (qidx[:, :4] + 128 - sidx)), 0.0)
    bd = np.zeros((128, 128), np.float32)
    bd[:64, :64] = 1.0
    bd[64:, 64:] = 1.0
    En = np.zeros((128, 4, 2, 2, 4, 16), np.float64)
    for kvh in range(4):
        for par in range(2):
            for c2 in range(2):
                m = slopes[4 * kvh + 2 * c2 + par]
                for ip in range(4):
                    for i in range(ip, 4):
                        for b in range(16):
                            En[64 + 16 * ip + b, kvh, par, c2, i, b] = np.exp(-m * (i - ip))
    sgn = np.ones((128, 2), np.float32); sgn[:64] = -1.0
    tt = np.arange(128) // 16
    mask_ts = (tt[None, :] >= tt[:, None]).astype(np.float32)
    kvals = np.concatenate([np.arange(7, -8, -1), np.arange(0, 9)]).astype(np.float32)
    jvals = (1.0 * (np.arange(64) + 1)).astype(np.float32)
    return {"Etab": E.reshape(128, -1).astype(np.float32), "ident": np.eye(128, dtype=np.float32), "bdones": bd, "Entab": En.reshape(128, -1).astype(np.float32), "Esamp": Es.reshape(128, -1).astype(np.float32),
            "sgn": sgn, "mask_ts": mask_ts, "kvals": np.tile(kvals, (128, 1)), "jvals": np.tile(jvals, (128, 1))}


def host_layouts(inp):
    f32 = lambda a: np.ascontiguousarray(np.asarray(a, dtype=np.float32))
    w_in = np.asarray(inp["w_in"], dtype=np.float32)
    parts = [w_in[:, :, 0:2048]]
    for kvh in range(4):
        kk = w_in[:, :, 2048 + kvh * 64:2048 + (kvh + 1) * 64]
        parts += [kk, kk]
    parts.append(w_in[:, :, 2304:2560])
    out = {"w_in_aug": f32(np.concatenate(parts, axis=-1))}
    gq = np.asarray(inp["q_norm"], dtype=np.float32)
    gk = np.asarray(inp["k_norm"], dtype=np.float32)
    gqk = np.zeros((128, DEPTH, 2), np.float32)
    for L in range(DEPTH):
        gqk[:, L, 0] = np.tile(gq[L], 2)
        gqk[:, L, 1] = np.tile(gk[L], 2)
    out["gqk"] = gqk.reshape(128, -1)
    sk = np.asarray(inp["sinks"], dtype=np.float32)
    sr = np.zeros((128, DEPTH, 4, 2), np.float32)
    for L in range(DEPTH):
        for kvh in range(4):
            for c2 in range(2):
                sr[:64, L, kvh, c2] = sk[L, 4 * kvh + 2 * c2]
                sr[64:, L, kvh, c2] = sk[L, 4 * kvh + 2 * c2 + 1]
    out["sinks_rep"] = sr.reshape(128, -1)
    for k in ["ffn1_w_gate", "ffn1_w_up", "ffn1_w_down", "ffn2_w_gate", "ffn2_w_up", "ffn2_w_down",
              "w_out", "ffn1_norm", "mix_norm", "ffn2_norm", "ssm_out_norm", "attn_out_norm"]:
        out[k] = f32(inp[k])
    tp = lambda a, ax: np.asarray(a, dtype=np.float32).transpose(ax)
    Are = tp(inp["ssm_A_re"], (0, 2, 1)); Aim = tp(inp["ssm_A_im"], (0, 2, 1))
    out["Are"] = f32(np.concatenate([Are, Are], 1)); out["Aim"] = f32(np.concatenate([Aim, Aim], 1))
    out["ldt"] = f32(np.broadcast_to(np.asarray(inp["ssm_log_dt"], dtype=np.float32)[:, None, :], (DEPTH, 128, 64)))
    Br = tp(inp["ssm_B_re"], (0, 2, 1, 3)).reshape(DEPTH, 64, 1024); Bi = tp(inp["ssm_B_im"], (0, 2, 1, 3)).reshape(DEPTH, 64, 1024)
    Cr = tp(inp["ssm_C_re"], (0, 3, 1, 2)).reshape(DEPTH, 64, 1024); Ci = tp(inp["ssm_C_im"], (0, 3, 1, 2)).reshape(DEPTH, 64, 1024)
    out["Bq1"] = f32(np.concatenate([Br, Bi], 1)); out["Bq2"] = f32(np.concatenate([Bi, Br], 1))
    out["Cq1"] = f32(np.concatenate([Cr, Ci], 1)); out["Cq2"] = f32(np.concatenate([Ci, Cr], 1))
    Dr = np.asarray(inp["ssm_D"], dtype=np.float32).reshape(DEPTH, 8, 128)
    out["Dv"] = f32(Dr.transpose(2, 0, 1).reshape(128, -1))
    wg = np.asarray(inp["ssm_w_glu"], dtype=np.float32)
    bdw = np.zeros((128, DEPTH, 8, 128), np.float32)
    for L in range(DEPTH):
        for g in range(64):
            c_, g8 = g // 8, g % 8
            bdw[16 * g8:16 * g8 + 16, L, c_, 16 * g8:16 * g8 + 16] = wg[L, g]
    out["wglu_bd"] = bdw.reshape(128, -1)
    out.update(host_consts())
    return out


def host_state(inp, c):
    re = np.asarray(inp["state_ssm_re"], dtype=np.float32)[:, c * NSEQ_S:(c + 1) * NSEQ_S]
    im = np.asarray(inp["state_ssm_im"], dtype=np.float32)[:, c * NSEQ_S:(c + 1) * NSEQ_S]
    rt, it = re.transpose(0, 3, 2, 1), im.transpose(0, 3, 2, 1)
    h0 = np.concatenate([rt, it], 1).reshape(DEPTH, 128, -1)
    h0sw = np.concatenate([it, rt], 1).reshape(DEPTH, 128, -1)
    return np.ascontiguousarray(h0), np.ascontiguousarray(h0sw)


def host_cache(inp, c):
    ck = np.asarray(inp["cache_k"], dtype=np.float32)[:, c * NSEQ_S:(c + 1) * NSEQ_S]
    cv = np.asarray(inp["cache_v"], dtype=np.float32)[:, c * NSEQ_S:(c + 1) * NSEQ_S]
    kt = ck.transpose(0, 4, 1, 3, 2)
    kcd = np.concatenate([kt, kt], 1).reshape(DEPTH, 128, -1)
    vt = cv.transpose(0, 2, 1, 3, 4)
    vcd = np.concatenate([vt, vt], -1).reshape(DEPTH, 128, -1)
    return {"kcd": np.ascontiguousarray(kcd), "vcd": np.ascontiguousarray(vcd),
            "ck_raw": np.ascontiguousarray(ck.reshape(DEPTH, NSEQ_S, 128, 256)),
            "cv_raw": np.ascontiguousarray(cv.reshape(DEPTH, NSEQ_S, 128, 256))}


_NC = None


def kernel(**inp):
    global _NC
    f32 = lambda a: np.ascontiguousarray(np.asarray(a, dtype=np.float32))
    if _NC is None:
        _NC = build()
    xp = np.asarray(inp["x_prompt"], dtype=np.float32)
    xs = np.asarray(inp["x_sample"], dtype=np.float32)
    shared = host_layouts(inp)
    in_maps = []
    for c in range(NCORES):
        m = dict(shared)
        m["xTp"] = f32(xp[c % NB].T)
        m["xTs"] = f32(xs[c * NSEQ_S:(c + 1) * NSEQ_S].transpose(1, 0, 2).reshape(NSEQ_S * LS, D).T)
        m["h0"], m["h0sw"] = host_state(inp, c)
        m.update(host_cache(inp, c))
        in_maps.append(m)
    res = run_bass_kernel_spmd(_NC, in_maps, core_ids=list(range(NCORES)))
    R = res.results
    y_prompt = np.stack([R[b]["yTp"].T for b in range(NB)]).astype(np.float32)
    y_sample = np.concatenate([R[c]["yTs"].T.reshape(LS, NSEQ_S, D).transpose(1, 0, 2) for c in range(NCORES)]).astype(np.float32)
    A = lambda a: np.asarray(a, dtype=np.float32)
    prompt_k = np.stack([A(R[b]["PK_D"]).reshape(DEPTH, 128, 4, 64) for b in range(NB)], axis=1)
    prompt_v = np.stack([A(R[b]["PV_D"]).reshape(DEPTH, 128, 4, 64) for b in range(NB)], axis=1)
    hp = np.stack([A(R[b]["HP_D"]) for b in range(NB)], axis=1)
    prompt_re = np.ascontiguousarray(hp[:, :, :64, :].transpose(0, 1, 3, 2))
    prompt_im = np.ascontiguousarray(hp[:, :, 64:, :].transpose(0, 1, 3, 2))
    sample_k = np.concatenate([A(R[c]["SK_D"]).reshape(DEPTH, NSEQ_S, 128, 4, 64) for c in range(NCORES)], axis=1)
    sample_v = np.concatenate([A(R[c]["SV_D"]).reshape(DEPTH, NSEQ_S, 128, 4, 64) for c in range(NCORES)], axis=1)
    hs = np.concatenate([A(R[c]["HS_D"]).reshape(DEPTH, 128, 64, NSEQ_S) for c in range(NCORES)], axis=3)
    sample_re = np.ascontiguousarray(hs[:, :64].transpose(0, 3, 2, 1))
    sample_im = np.ascontiguousarray(hs[:, 64:].transpose(0, 3, 2, 1))
    return (y_prompt, y_sample, prompt_k, prompt_v, prompt_re, prompt_im, sample_k, sample_v, sample_re, sample_im)
```
